# Optimizing a Trainium2 kernel written in Bass

```python
import math
import jax, jax.numpy as jnp
from jax import lax
import numpy as np

D_MODEL = 1024
BATCH = 8
SEQ = 2048
DEPTH = 2
DEC_BATCH = 128
DEC_SEQ = 4
PAST_LEN = 16384
PAGE_SIZE = 128

RW_HEADS = 4
RW_DIM = 64
RW_WIDTH = RW_HEADS * RW_DIM
RW_DECAY_LORA = 64
RW_AAA_LORA = 64
RW_GATE_LORA = 128
RW_PROJ = 3 * RW_WIDTH + RW_DECAY_LORA + RW_AAA_LORA + RW_GATE_LORA
RW_SPLITS = [RW_WIDTH, 2 * RW_WIDTH, 3 * RW_WIDTH, 3 * RW_WIDTH + RW_DECAY_LORA,
             3 * RW_WIDTH + RW_DECAY_LORA + RW_AAA_LORA]
RW_GN_EPS = 64e-5
HG_HEADS = 4
HG_DIM = 128
HG_WIDTH = HG_HEADS * HG_DIM
HG_PROJ = 4 * HG_WIDTH
HG_CHUNK = 16
RMS_EPS = 1e-6
F_MIN = 1e-30
CM_HEADS = 4
CM_DIM = 64
CM_WIDTH = CM_HEADS * CM_DIM
CM_CHUNK = 128
CM_PROJ = 2 * CM_WIDTH
MIX_WIDTH = RW_WIDTH + HG_WIDTH + CM_WIDTH
IN_PROJ = RW_PROJ + HG_PROJ + CM_PROJ
D_FF = 2816
LN_EPS = 1e-5
ALPHA = (2.0 * DEPTH) ** 0.25
BETA = (8.0 * DEPTH) ** -0.25

kernel_name = "hybrid_rwkv7_hgrn2_chunkgmlp_decoder_step"


def layer_norm(x, g, b, eps=LN_EPS):
    xf = x.astype(jnp.float32)
    mu = xf.mean(-1, keepdims=True)
    var = jnp.square(xf - mu).mean(-1, keepdims=True)
    return ((xf - mu) * lax.rsqrt(var + eps) * g + b).astype(x.dtype)


def swiglu(x, w_in, w_out):
    gate, up = jnp.split(x @ w_in, 2, axis=-1)
    return (jax.nn.silu(gate) * up) @ w_out


def rwkv7_mix(z, shift0, S0, mu, w0, w_w2, a0, a_w2, g_w2, k_k, k_a, r_k, gn_g, gn_b):
    B, T, _ = z.shape
    f32 = jnp.float32
    z_prev = jnp.concatenate([shift0[:, None, :].astype(z.dtype), z[:, :-1]], axis=1)
    zs = (z + (z_prev - z) * mu).astype(f32)
    r, k, v, xw, xa, xg = jnp.split(zs, RW_SPLITS, axis=-1)
    w_log = -jax.nn.softplus(-(w0 + jnp.tanh(xw) @ w_w2)) - 0.5
    decay = jnp.exp(-jnp.exp(w_log))
    a = jax.nn.sigmoid(a0 + xa @ a_w2)
    g = jax.nn.sigmoid(xg) @ g_w2
    kk = k * k_k
    k = k * (1.0 + (a - 1.0) * k_a)
    hs = lambda t: t.reshape(B, T, RW_HEADS, RW_DIM)
    r, decay, k, v, kk, a = map(hs, (r, decay, k, v, kk, a))
    kk = kk / jnp.maximum(jnp.sqrt(jnp.sum(kk * kk, axis=-1, keepdims=True)), 1e-12)

    def step(S, inp):
        r_t, w_t, k_t, v_t, kk_t, a_t = inp
        sa = jnp.einsum('bhvk,bhk->bhv', S, -kk_t)
        S = (S * w_t[:, :, None, :] + sa[..., None] * (kk_t * a_t)[:, :, None, :]
             + v_t[..., None] * k_t[:, :, None, :])
        return S, jnp.einsum('bhvk,bhk->bhv', S, r_t)

    tm = lambda t: jnp.moveaxis(t, 1, 0)
    S_T, o = lax.scan(step, S0.astype(f32), tuple(map(tm, (r, decay, k, v, kk, a))))
    o = jnp.moveaxis(o, 0, 1)
    o_mu = o.mean(-1, keepdims=True)
    o_var = jnp.square(o - o_mu).mean(-1, keepdims=True)
    o = ((o - o_mu) * lax.rsqrt(o_var + RW_GN_EPS)).reshape(B, T, RW_WIDTH) * gn_g + gn_b
    o = o + (jnp.sum(r * k * r_k, axis=-1, keepdims=True) * v).reshape(B, T, RW_WIDTH)
    o = o * g
    return o.astype(z.dtype), S_T.astype(S0.dtype), z[:, -1]


def gated_linear_recurrence(q, k, v, log_f, S0):
    B, T, H, K = q.shape
    C = math.gcd(T, HG_CHUNK)
    N = T // C
    mask = jnp.tril(jnp.ones((C, C), dtype=bool))[None, :, :, None, None]
    to_chunks = lambda t: jnp.moveaxis(t.reshape(B, N, C, H, t.shape[-1]), 1, 0)

    def step(S, inp):
        qc, kc, vc, gc = inp
        cum = jnp.cumsum(gc, axis=1)
        diff = cum[:, :, None] - cum[:, None, :]
        dec = jnp.where(mask, jnp.exp(jnp.minimum(diff, 0.0)), 0.0)
        A = jnp.einsum('bthk,bshk,btshk->bhts', qc, kc, dec)
        o = (jnp.einsum('bhts,bshv->bthv', A, vc)
             + jnp.einsum('bthk,bhkv->bthv', qc * jnp.exp(cum), S))
        total = cum[:, -1]
        S = (jnp.exp(total)[..., None] * S
             + jnp.einsum('bshk,bshv->bhkv', kc * jnp.exp(total[:, None] - cum), vc))
        return S, o

    S_T, o = lax.scan(step, S0, (to_chunks(q), to_chunks(k), to_chunks(v), to_chunks(log_f)))
    return jnp.moveaxis(o, 0, 1).reshape(B, T, H, v.shape[-1]), S_T


def hgrn2_mix(z, S0, lb, norm_g):
    B, T, _ = z.shape
    f32 = jnp.float32
    q, fz, i, og = jnp.split(z.astype(f32), 4, axis=-1)
    lbf = lb.astype(f32)
    f = lbf + (1.0 - lbf) * jax.nn.sigmoid(fz)
    log_f = jnp.log(jnp.maximum(f, F_MIN))
    k = 1.0 - f
    hs = lambda t: t.reshape(B, T, HG_HEADS, HG_DIM)
    o, S_T = gated_linear_recurrence(hs(jax.nn.silu(q)), hs(k), hs(i), hs(log_f), S0.astype(f32))
    o = o * lax.rsqrt(jnp.mean(o * o, axis=-1, keepdims=True) + RMS_EPS)
    o = o.reshape(B, T, HG_WIDTH) * norm_g * jax.nn.silu(og)
    return o.astype(z.dtype), S_T.astype(S0.dtype)


def chunk_mlp_mix(z, ws, bs, ln_g, ln_b):
    B, T, _ = z.shape
    u, v = jnp.split(z, 2, axis=-1)
    u = jax.nn.gelu(u, approximate=False)
    v = jax.nn.gelu(v, approximate=False).reshape(B, T, CM_HEADS, CM_DIM)
    v = layer_norm(v, ln_g.reshape(CM_HEADS, CM_DIM), ln_b.reshape(CM_HEADS, CM_DIM))
    Tp = -(-T // CM_CHUNK) * CM_CHUNK
    vp = jnp.pad(v, ((0, 0), (0, Tp - T), (0, 0), (0, 0))).reshape(B, Tp // CM_CHUNK, CM_CHUNK, CM_HEADS, CM_DIM)
    w_causal = ws * jnp.tril(jnp.ones((CM_CHUNK, CM_CHUNK), ws.dtype))
    mixed = jnp.einsum('hts,bnshd->bnthd', w_causal, vp) + bs.T[None, None, :, :, None]
    mixed = mixed.reshape(B, Tp, CM_HEADS, CM_DIM)[:, :T].reshape(B, T, CM_WIDTH)
    return u * mixed, v.reshape(B, T, CM_WIDTH)


def hgrn_lower_bounds(logits):
    s = jax.nn.softmax(logits.astype(jnp.float32), axis=0)
    return jnp.cumsum(s, axis=0) - s[0]


def run_trunk(x, rw_S0, rw_shift0, hg_S0, p):
    lb = hgrn_lower_bounds(p['hg_lb_logits'])
    rw_S, rw_sh, hg_S, cm_v = [], [], [], []
    for l in range(DEPTH):
        x = layer_norm(ALPHA * x + 0.5 * swiglu(x, p['ffn1_w_in'][l], p['ffn1_w_out'][l]),
                       p['ln1_g'][l], p['ln1_b'][l])
        z = x @ p['mix_w_in'][l]
        z_rw, z_hg, z_cm = jnp.split(z, [RW_PROJ, RW_PROJ + HG_PROJ], axis=-1)
        o_rw, s_rw, sh_rw = rwkv7_mix(z_rw, rw_shift0[l], rw_S0[l], p['rw_mu'][l], p['rw_w0'][l],
                                      p['rw_w_w2'][l], p['rw_a0'][l], p['rw_a_w2'][l], p['rw_g_w2'][l],
                                      p['rw_k_k'][l], p['rw_k_a'][l], p['rw_r_k'][l],
                                      p['rw_gn_g'][l], p['rw_gn_b'][l])
        o_hg, s_hg = hgrn2_mix(z_hg, hg_S0[l], lb[l], p['hg_norm_g'][l])
        o_cm, v_cm = chunk_mlp_mix(z_cm, p['cm_ws'][l], p['cm_bs'][l], p['cm_ln_g'][l], p['cm_ln_b'][l])
        mix = jnp.concatenate([o_rw, o_hg, o_cm], axis=-1) @ p['mix_w_out'][l]
        x = layer_norm(ALPHA * x + mix, p['ln2_g'][l], p['ln2_b'][l])
        x = layer_norm(ALPHA * x + 0.5 * swiglu(x, p['ffn2_w_in'][l], p['ffn2_w_out'][l]),
                       p['ln3_g'][l], p['ln3_b'][l])
        rw_S.append(s_rw)
        rw_sh.append(sh_rw)
        hg_S.append(s_hg)
        cm_v.append(v_cm)
    return x, jnp.stack(rw_S), jnp.stack(rw_sh), jnp.stack(hg_S), jnp.stack(cm_v)


def setup_inputs(seed: int = 0) -> dict:
    key = jax.random.key(seed)
    ks = iter(jax.random.split(key, 48))
    nrm = lambda shape, s: jax.random.normal(next(ks), shape, jnp.float32) * s
    L = DEPTH
    d = {}
    d['x_prompt'] = nrm((BATCH, SEQ, D_MODEL), 1.0)
    d['x_sample'] = nrm((DEC_BATCH, DEC_SEQ, D_MODEL), 1.0)
    d['state_rwkv'] = nrm((L, DEC_BATCH, RW_HEADS, RW_DIM, RW_DIM), 0.3)
    d['state_rwkv_shift'] = nrm((L, DEC_BATCH, RW_PROJ), 1.0)
    d['state_hgrn'] = nrm((L, DEC_BATCH, HG_HEADS, HG_DIM, HG_DIM), 0.5)
    d['ffn1_w_in'] = nrm((L, D_MODEL, 2 * D_FF), D_MODEL ** -0.5)
    d['ffn1_w_out'] = nrm((L, D_FF, D_MODEL), BETA * D_FF ** -0.5)
    d['ln1_g'] = 1.0 + nrm((L, D_MODEL), 0.02)
    d['ln1_b'] = nrm((L, D_MODEL), 0.02)
    d['mix_w_in'] = nrm((L, D_MODEL, IN_PROJ), D_MODEL ** -0.5)
    d['mix_w_out'] = nrm((L, MIX_WIDTH, D_MODEL), BETA * MIX_WIDTH ** -0.5)
    d['ln2_g'] = 1.0 + nrm((L, D_MODEL), 0.02)
    d['ln2_b'] = nrm((L, D_MODEL), 0.02)
    d['rw_mu'] = jax.random.uniform(next(ks), (L, RW_PROJ), jnp.float32)
    d['rw_w0'] = -1.0 + nrm((L, RW_WIDTH), 0.5)
    d['rw_w_w2'] = nrm((L, RW_DECAY_LORA, RW_WIDTH), 0.1 * RW_DECAY_LORA ** -0.5)
    d['rw_a0'] = nrm((L, RW_WIDTH), 0.1)
    d['rw_a_w2'] = nrm((L, RW_AAA_LORA, RW_WIDTH), 0.1 * RW_AAA_LORA ** -0.5)
    d['rw_g_w2'] = nrm((L, RW_GATE_LORA, RW_WIDTH), RW_GATE_LORA ** -0.5)
    d['rw_k_k'] = 0.85 + nrm((L, RW_WIDTH), 0.02)
    d['rw_k_a'] = 1.0 + nrm((L, RW_WIDTH), 0.02)
    d['rw_r_k'] = nrm((L, RW_HEADS, RW_DIM), 0.1)
    d['rw_gn_g'] = 1.0 + nrm((L, RW_WIDTH), 0.02)
    d['rw_gn_b'] = nrm((L, RW_WIDTH), 0.02)
    d['hg_lb_logits'] = nrm((L, HG_WIDTH), 0.5)
    d['hg_norm_g'] = 1.0 + nrm((L, HG_WIDTH), 0.02)
    d['cm_ws'] = nrm((L, CM_HEADS, CM_CHUNK, CM_CHUNK), CM_CHUNK ** -0.5)
    d['cm_bs'] = 1.0 + nrm((L, CM_HEADS, CM_CHUNK), 0.1)
    d['cm_ln_g'] = 1.0 + nrm((L, CM_WIDTH), 0.02)
    d['cm_ln_b'] = nrm((L, CM_WIDTH), 0.02)
    d['ffn2_w_in'] = nrm((L, D_MODEL, 2 * D_FF), D_MODEL ** -0.5)
    d['ffn2_w_out'] = nrm((L, D_FF, D_MODEL), BETA * D_FF ** -0.5)
    d['ln3_g'] = 1.0 + nrm((L, D_MODEL), 0.02)
    d['ln3_b'] = nrm((L, D_MODEL), 0.02)
    return d


def reference(x_prompt, x_sample, state_rwkv, state_rwkv_shift, state_hgrn,
              ffn1_w_in, ffn1_w_out, ln1_g, ln1_b, mix_w_in, mix_w_out, ln2_g, ln2_b,
              rw_mu, rw_w0, rw_w_w2, rw_a0, rw_a_w2, rw_g_w2, rw_k_k, rw_k_a, rw_r_k,
              rw_gn_g, rw_gn_b, hg_lb_logits, hg_norm_g, cm_ws, cm_bs, cm_ln_g, cm_ln_b,
              ffn2_w_in, ffn2_w_out, ln3_g, ln3_b):
    p = dict(ffn1_w_in=ffn1_w_in, ffn1_w_out=ffn1_w_out, ln1_g=ln1_g, ln1_b=ln1_b,
             mix_w_in=mix_w_in, mix_w_out=mix_w_out, ln2_g=ln2_g, ln2_b=ln2_b,
             rw_mu=rw_mu, rw_w0=rw_w0, rw_w_w2=rw_w_w2, rw_a0=rw_a0, rw_a_w2=rw_a_w2,
             rw_g_w2=rw_g_w2, rw_k_k=rw_k_k, rw_k_a=rw_k_a, rw_r_k=rw_r_k,
             rw_gn_g=rw_gn_g, rw_gn_b=rw_gn_b, hg_lb_logits=hg_lb_logits, hg_norm_g=hg_norm_g,
             cm_ws=cm_ws, cm_bs=cm_bs, cm_ln_g=cm_ln_g, cm_ln_b=cm_ln_b,
             ffn2_w_in=ffn2_w_in, ffn2_w_out=ffn2_w_out, ln3_g=ln3_g, ln3_b=ln3_b)
    B = x_prompt.shape[0]
    rw_S0 = jnp.zeros((DEPTH, B, RW_HEADS, RW_DIM, RW_DIM), state_rwkv.dtype)
    rw_sh0 = jnp.zeros((DEPTH, B, RW_PROJ), state_rwkv_shift.dtype)
    hg_S0 = jnp.zeros((DEPTH, B, HG_HEADS, HG_DIM, HG_DIM), state_hgrn.dtype)
    y_prompt, rw_S_p, rw_sh_p, hg_S_p, _ = run_trunk(x_prompt, rw_S0, rw_sh0, hg_S0, p)
    y_sample, rw_S_s, rw_sh_s, hg_S_s, cm_v_s = run_trunk(x_sample, state_rwkv, state_rwkv_shift,
                                                         state_hgrn, p)
    return (y_prompt, y_sample, rw_S_p, rw_sh_p, hg_S_p, rw_S_s, rw_sh_s, hg_S_s, cm_v_s)
```

```python
import math
from contextlib import ExitStack
import numpy as np
import concourse.bass as bass
import concourse.mybir as mybir
from concourse.bass_utils import run_bass_kernel_spmd

F32 = mybir.dt.float32
BF16 = mybir.dt.bfloat16
F32R = mybir.dt.float32r
AF = mybir.ActivationFunctionType
ALU = mybir.AluOpType

NCORES = 8
D = 1024
KC = 8
SEQ = 2048
NSB = 16
NST = 4
L = 2
DFF = 2816
NJ = 22
INP = 3584
ALPHA = (2.0 * L) ** 0.25
LN_EPS = 1e-5
GN_EPS = 64e-5
RMS_EPS = 1e-6
WSCALE = -math.exp(-0.5)
TT = [(0, 512), (512, 512), (1024, 512), (1536, 512), (2048, 64)]
NTOK = 2112
MT = [(ti, off, 256) for ti in range(4) for off in (0, 256)] + [(4, 0, 64)]
ROUNDS = [(0, 6), (6, 6), (12, 5), (17, 5)]

COMPUTE = ("pe", "act", "dve", "pool")
ALLENG = ("pe", "act", "dve", "pool", "sp")

PCOL_SPEC = [("ln1_g", 8), ("ln1_b", 8), ("ln2_g", 8), ("ln2_b", 8), ("ln3_g", 8), ("ln3_b", 8),
             ("rw_mu", 8), ("rw_w0", 2), ("rw_a0", 2), ("rw_k_k", 2), ("rw_k_a", 2), ("rw_r_k", 2),
             ("rw_gn_g", 2), ("rw_gn_b", 2), ("hg_lb_logits", 4), ("hg_norm_g", 4),
             ("cm_ln_g", 2), ("cm_ln_b", 2)]
PCOL_OFF = {}
_o = 0
for _n, _c in PCOL_SPEC:
    PCOL_OFF[_n] = (_o, _c)
    _o += L * _c
NPCOL = _o


def pcol_idx(name, l, c):
    o, n = PCOL_OFF[name]
    return o + l * n + c


class Tok:
    __slots__ = ("sem", "val", "know")

    def __init__(self, sem, val, know):
        self.sem, self.val, self.know = sem, val, know


class Buf:
    __slots__ = ("name", "w", "r", "excl")

    def __init__(self, name="", excl=False):
        self.name, self.w, self.r, self.excl = name, None, [], excl


class T:
    __slots__ = ("ap", "buf")

    def __init__(self, ap, buf):
        self.ap, self.buf = ap, buf

    def __getitem__(self, k):
        return T(self.ap[k], self.buf)

    def re(self, s, **kw):
        return T(self.ap.rearrange(s, **kw), self.buf)

    def bc(self, shape):
        return T(self.ap.broadcast_to(shape), self.buf)

    def bitcast(self, dt):
        return T(self.ap.bitcast(dt), self.buf)


class Sched:
    def __init__(self, nc, esems, dsems):
        self.nc = nc
        self.prog = {e: [] for e in ALLENG}
        self.esem = esems
        self.cnt = {e: 0 for e in COMPUTE}
        self.know = {e: {} for e in ALLENG}
        self.dpool = [[s, 0, None] for s in dsems]
        self.drr = 0
        self.n_ops = 0
        self.n_waits = 0
        self.defer = None

    def _learn(self, eng, tok):
        kn = self.know[eng]
        if kn.get(tok.sem.num, 0) < tok.val:
            kn[tok.sem.num] = tok.val
        for k, v in tok.know.items():
            if kn.get(k, 0) < v:
                kn[k] = v

    def _need(self, eng, reads, writes):
        need = {}
        mysem = self.esem[eng].num if eng in self.esem else None

        def req(tok, raw):
            if tok is None:
                return
            if (not raw) and tok.sem.num == mysem:
                return
            k = tok.sem.num
            if k not in need or need[k].val < tok.val:
                need[k] = tok

        for b in reads:
            req(b.w, True)
            if b.excl:
                for r in b.r:
                    req(r, False)
        for b in writes:
            req(b.w, False)
            for r in b.r:
                req(r, False)
        waits = []
        kn = self.know[eng]
        for k, tok in need.items():
            if kn.get(k, 0) >= tok.val:
                continue
            waits.append((tok.sem, tok.val))
            self._learn(eng, tok)
        return waits

    def op(self, eng, fn, reads=(), writes=(), dma=False, cost=0.3, single=True):
        if self.defer is not None:
            self.defer.append((eng, fn, [b for b in reads if b is not None], [b for b in writes if b is not None], dma, cost, single))
            return None
        reads = [b for b in reads if b is not None]
        writes = [b for b in writes if b is not None]
        waits = self._need(eng, reads, writes)
        if dma:
            slot = self.dpool[self.drr % len(self.dpool)]
            self.drr += 1
            sem, val, last = slot
            if last is not None and self.know[eng].get(sem.num, 0) < last.val:
                waits.append((sem, last.val))
                self._learn(eng, last)
            val += 16
            slot[1] = val
            tok = Tok(sem, val, dict(self.know[eng]))
            slot[2] = tok
            inc = 16
        else:
            self.cnt[eng] += 1
            sem = self.esem[eng]
            val = self.cnt[eng]
            self.know[eng][sem.num] = val
            tok = Tok(sem, val, dict(self.know[eng]))
            inc = 1
        self.n_ops += 1
        self.n_waits += len(waits)

        aw = getattr(fn, "accepts_wait", False) and (not dma) and len(waits) > 0
        embed = (not aw) and single and (not dma) and len(waits) > 0

        def emit(E, waits=waits, fn=fn, sem=sem, inc=inc, embed=embed, aw=aw):
            pre = waits[:-1] if (embed or aw) else waits
            for (s, v) in pre:
                E.wait_ge(s, v)
            if aw:
                ins = fn(E, waits[-1])
            else:
                ins = fn(E)
                if embed:
                    ins._wait_ge(waits[-1][0], waits[-1][1])
            ins.then_inc(sem, inc)

        self.prog[eng].append(emit)
        for b in reads:
            b.r.append(tok)
            if len(b.r) > 64:
                b.r = b.r[-64:] if False else b.r
        for b in writes:
            b.w = tok
            b.r = []
        return tok

    def barrier(self):
        toks = []
        for slot in self.dpool:
            if slot[2] is not None:
                toks.append(slot[2])
        for e in COMPUTE:
            if self.cnt[e] > 0:
                toks.append(Tok(self.esem[e], self.cnt[e], {}))
        for eng in ALLENG:
            waits = []
            for tok in toks:
                if self.know[eng].get(tok.sem.num, 0) < tok.val:
                    waits.append((tok.sem, tok.val))
                    self.know[eng][tok.sem.num] = tok.val

            def emit(E, waits=waits):
                for (s, v) in waits:
                    E.wait_ge(s, v)

            if waits:
                self.prog[eng].append(emit)

    def replay(self, block):
        prog = self.prog

        @block.tensor
        def _(E):
            for f in prog["pe"]:
                f(E)

        @block.scalar
        def _(E):
            for f in prog["act"]:
                f(E)

        @block.vector
        def _(E):
            for f in prog["dve"]:
                f(E)

        @block.gpsimd
        def _(E):
            for f in prog["pool"]:
                f(E)

        @block.sync
        def _(E):
            for f in prog["sp"]:
                f(E)


class K:
    def __init__(self, S):
        self.S = S

    @staticmethod
    def _ap(x):
        return x.ap if isinstance(x, T) else x

    @staticmethod
    def _bufs(*xs):
        return [x.buf for x in xs if isinstance(x, T)]

    @staticmethod
    def _cost(eng, out):
        n = 1
        for d in out.ap.shape[1:]:
            n *= int(d)
        if eng == "pool":
            return 0.15 + n / 300.0
        return 0.15 + n / 1000.0

    def tt(self, eng, out, a, b, op):
        self.S.op(eng, lambda E: E.tensor_tensor(out=out.ap, in0=a.ap, in1=b.ap, op=op),
                  reads=self._bufs(a, b), writes=[out.buf], cost=self._cost(eng, out))

    def ts(self, eng, out, a, s1, s2, op0, op1=None):
        s1a, s2a = self._ap(s1), self._ap(s2)
        if op1 is None:
            fn = lambda E: E.tensor_scalar(out=out.ap, in0=a.ap, scalar1=s1a, scalar2=None, op0=op0)
        else:
            fn = lambda E: E.tensor_scalar(out=out.ap, in0=a.ap, scalar1=s1a, scalar2=s2a, op0=op0, op1=op1)
        self.S.op(eng, fn, reads=self._bufs(a, s1, s2), writes=[out.buf], cost=self._cost(eng, out))

    def stt(self, eng, out, a, s, b, op0, op1):
        sa = self._ap(s)
        self.S.op(eng, lambda E: E.scalar_tensor_tensor(out=out.ap, in0=a.ap, scalar=sa, in1=b.ap, op0=op0, op1=op1),
                  reads=self._bufs(a, s, b), writes=[out.buf], cost=self._cost(eng, out))

    def copy(self, eng, out, a):
        if eng == "act":
            fn = lambda E: E.activation(out=out.ap, in_=a.ap, func=AF.Identity)
        else:
            fn = lambda E: E.tensor_copy(out=out.ap, in_=a.ap)
        self.S.op(eng, fn, reads=[a.buf], writes=[out.buf], cost=self._cost(eng, out))

    def act(self, out, a, func, bias=None, scale=1.0):
        ba, sa = self._ap(bias), self._ap(scale)
        if bias is None:
            fn = lambda E: E.activation(out=out.ap, in_=a.ap, func=func, scale=sa)
        else:
            fn = lambda E: E.activation(out=out.ap, in_=a.ap, func=func, bias=ba, scale=sa)
        self.S.op("act", fn, reads=self._bufs(a, bias, scale), writes=[out.buf], cost=self._cost("act", out))

    def memset(self, eng, out, val):
        self.S.op(eng, lambda E: E.memset(out.ap, val), writes=[out.buf], cost=self._cost(eng, out))

    def scan(self, out, d0, d1):
        self.S.op("dve", lambda E: E.tensor_tensor_scan(out=out.ap, data0=d0.ap, data1=d1.ap, initial=0.0,
                                                        op0=ALU.mult, op1=ALU.add),
                  reads=[d0.buf, d1.buf], writes=[out.buf], cost=self._cost("dve", out))

    def mm(self, out, pairs, extra_reads=(), tp=None):
        n = len(pairs)

        def fn(E, w=None):
            ins = None
            for i, (l, r) in enumerate(pairs):
                if tp is None:
                    ins = E.matmul(out.ap, lhsT=l.ap, rhs=r.ap, start=(i == 0), stop=(i == n - 1))
                else:
                    ins = E.matmul(out.ap, lhsT=l.ap, rhs=r.ap, start=(i == 0), stop=(i == n - 1), tile_position=tp)
                if i == 0 and w is not None:
                    ins._wait_ge(w[0], w[1])
            return ins

        fn.accepts_wait = True

        rd = []
        for (l, r) in pairs:
            rd += [l.buf, r.buf]
        ncol = int(out.ap.shape[-1])
        self.S.op("pe", fn, reads=rd + list(extra_reads), writes=[out.buf], cost=n * (0.06 + ncol / 2400.0), single=(n == 1))

    def tr(self, out, a, ident):
        self.S.op("pe", lambda E: E.transpose(out.ap, a.ap, ident.ap), reads=[a.buf, ident.buf], writes=[out.buf], cost=0.25)

    def dma(self, out, a, eng="sp", slow=False):
        oa, ia = self._ap(out), self._ap(a)
        if slow:
            fn = lambda E: E.dma_start(out=oa, in_=ia, allow_slow_non_contiguous=True)
        else:
            fn = lambda E: E.dma_start(out=oa, in_=ia)
        self.S.op(eng, fn, reads=self._bufs(a), writes=self._bufs(out), dma=True, cost=2.5)


def host_consts():
    c = {}
    c["ident"] = np.eye(128, dtype=np.float32)
    s = np.arange(64)[:, None]
    t = np.arange(64)[None, :]
    strict = (s < t).astype(np.float32)
    incl = (s <= t).astype(np.float32)
    same = ((s // 4) == (t // 4)).astype(np.float32)
    gm = np.stack([np.concatenate([strict, incl, strict, incl], 1),
                   np.concatenate([strict * same, incl * same, strict * same, incl * same], 1)], 0)
    c["gmask"] = gm.astype(np.float32)
    c["ntmask"] = np.stack([strict.T, (strict * same).T], 0).astype(np.float32)
    s2 = np.arange(128)[:, None]
    t2 = np.arange(128)[None, :]
    c["cmask"] = (s2 <= t2).astype(np.float32)
    c["cmask_s"] = (incl * same).astype(np.float32)
    rm = np.ones((128, 512), np.float32)
    rm[:, 0::64] = 0.0
    c["rmask"] = rm
    rs = np.ones((128, 64), np.float32)
    rs[:, 0::4] = 0.0
    c["rmask_s"] = rs
    cm = np.zeros((128, 16, 64), np.float32)
    for b in range(16):
        cm[:, b, 4 * b:4 * b + 4] = 1.0
    c["colmask"] = cm.reshape(128, 1024)
    rw = np.zeros((64, 16), np.float32)
    for b in range(16):
        rw[4 * b:4 * b + 4, b] = 1.0
    c["rowmask"] = rw
    blk = np.zeros((128, 128), np.float32)
    blk[:64, :64] = 1.0
    blk[64:, 64:] = 1.0
    c["blk1"] = blk
    c["e4"] = np.tile(np.eye(4, dtype=np.float32), (1, 16))
    return c


CONST_SHAPES = {"ident": [128, 128], "gmask": [2, 64, 256], "ntmask": [2, 64, 64], "cmask": [128, 128],
                "cmask_s": [64, 64], "rmask": [128, 512], "rmask_s": [128, 64], "colmask": [128, 1024],
                "rowmask": [64, 16], "blk1": [128, 128], "e4": [4, 64]}

IN_SHAPES = {
    "xp": [SEQ, D], "xs": [NSB * NST, D], "st_rw": [L, NSB, 4, 64, 64], "st_sh": [L, NSB, D],
    "st_hg": [L, NSB, 4, 128, 128],
    "ffn1_w_in": [L, D, 2 * DFF], "ffn1_w_out": [L, DFF, D], "mix_w_in": [L, D, INP], "mix_w_out": [L, D, D],
    "ffn2_w_in": [L, D, 2 * DFF], "ffn2_w_out": [L, DFF, D],
    "rw_w_w2": [L, 64, 256], "rw_a_w2": [L, 64, 256], "rw_g_w2": [L, 128, 256],
    "cm_ws": [L, 4, 128, 128], "cm_bs": [L, 4, 128], "pcols": [128, NPCOL],
}
OUT_SHAPES = {
    "y_p": [SEQ, D], "y_s": [NSB * NST, D], "rw_S_p": [L, 4, 64, 64], "rw_sh_p": [L, D],
    "hg_S_p": [L, 4, 128, 128], "rw_S_s": [L, NSB, 4, 64, 64], "rw_sh_s": [L, NSB, D],
    "hg_S_s": [L, NSB, 4, 128, 128], "cm_v_s": [L, NSB, NST, 256],
}


def build(taps=None, stop_after=None):
    taps = taps or {}
    nc = bass.Bass("TRN2", target_bir_lowering=False)
    I = {n: nc.dram_tensor(n, s, F32, kind="ExternalInput").ap() for n, s in IN_SHAPES.items()}
    Cn = {n: nc.dram_tensor("c_" + n, s, F32, kind="ExternalInput").ap() for n, s in CONST_SHAPES.items()}
    O = {n: nc.dram_tensor(n, s, F32, kind="ExternalOutput").ap() for n, s in OUT_SHAPES.items()}
    TAPO = {n: nc.dram_tensor("tap_" + n, s, F32, kind="ExternalOutput").ap() for n, s in taps.items()}

    with ExitStack() as st:
        def sb(name, shape, dt=F32, parts=128):
            h = st.enter_context(nc.sbuf_tensor("sb_" + name, [parts] + list(shape), dt))
            return T(h[:], Buf(name))

        x = [sb(f"x{t}", [KC, n]) for t, (_, n) in enumerate(TT)]
        xb = [sb(f"xb{t}", [KC, n], BF16) for t, (_, n) in enumerate(TT)]
        RCOLS = 17664
        Rh = st.enter_context(nc.sbuf_tensor("arena", [128, RCOLS], F32))
        stg = [sb(f"stg{i}", [1024]) for i in range(2)]
        wbf = [sb(f"wbf{i}", [1024], BF16) for i in range(6)]
        pcols = sb("pcols", [NPCOL])
        ident = sb("ident", [128])
        gmask = sb("gmask", [2, 256])
        ntmask = sb("ntmask", [2, 64])
        cmask = sb("cmask", [128])
        cmask_s = sb("cmask_s", [64], parts=64)
        rmask = sb("rmask", [256])
        rmask_s = sb("rmask_s", [64])
        colmask = sb("colmask", [16, 64], BF16)
        rowmask = sb("rowmask", [16], parts=64)
        blk1f = sb("blk1f", [128])
        blk1 = sb("blk1", [128], BF16)
        blk64 = sb("blk64", [128], BF16)
        onesD = sb("onesD", [128], BF16)
        onesM = sb("onesM", [128], BF16)
        ones_row = sb("ones_row", [128], BF16, parts=1)
        e4 = sb("e4", [64], parts=4)
        epsc = sb("epsc", [8])
        pcA = sb("pcA", [2 * L * 8])
        lbc = sb("lbc", [L * 4])
        omlc = sb("omlc", [L * 4])
        wa2 = sb("wa2", [L, 256], BF16)
        gw2 = sb("gw2", [L, 256], BF16)
        wct = sb("wct", [L * 4, 128], BF16)
        wcs = sb("wcs", [L * 4, 64], BF16, parts=64)
        bsrow = sb("bsrow", [L * 4 * 128], BF16, parts=1)
        bss = sb("bss", [L * 4, 64], BF16, parts=1)
        psh = [st.enter_context(nc.psum_tensor(f"ps{i}", [128, 512], F32)) for i in range(8)]
        ps = [T(h[:], Buf(f"ps{i}", excl=True)) for i, h in enumerate(psh)]
        esems = {e: st.enter_context(nc.semaphore(f"s_{e}")) for e in COMPUTE}
        dsems = [st.enter_context(nc.semaphore(f"d{i}")) for i in range(24)]
        block = st.enter_context(nc.Block())
        S = Sched(nc, esems, dsems)
        k = K(S)

        import os

        class _Stop(Exception):
            pass

        state = {"ps": 0, "stg": 0, "wbf": 0, "ar": 0, "eng": 0}

        def psum():
            sub = state.get("banks")
            if sub is None:
                state["ps"] += 1
                return ps[state["ps"] % 8]
            ln = state["lane"]
            lst = sub[ln]
            key = ("psl", ln)
            state[key] = state.get(key, -1) + 1
            return ps[lst[state[key] % len(lst)]]

        def arena_reset():
            S.barrier()
            state["ar"] = 0

        def ar(shape, dt=F32, parts=128, name="ar"):
            nfree = int(np.prod(shape))
            nbytes = nfree * (4 if dt == F32 else 2)
            cols = (nbytes + 63) // 64 * 16
            a = state["ar"]
            assert a + cols <= RCOLS, ("arena overflow", a, cols)
            state["ar"] = a + cols
            ap = Rh[0:parts, a:a + cols]
            if dt != F32:
                ap = ap.bitcast(dt)
            ap = ap[:, 0:nfree]
            if len(shape) == 2:
                ap = ap.rearrange("p (a b) -> p a b", a=shape[0])
            elif len(shape) == 3:
                ap = ap.rearrange("p (a b c) -> p a b c", a=shape[0], b=shape[1])
            return T(ap, Buf(name))

        def pc(name, l, c, parts=slice(0, 128)):
            i = pcol_idx(name, l, c)
            return pcols[parts, i:i + 1]

        def tap(name, t):
            if name in TAPO:
                k.dma(TAPO[name], t)

        def load_w(src, a, b):
            sg_ = stg[state["stg"] % 2]
            state["stg"] += 1
            wb_ = wbf[state["wbf"] % 6]
            state["wbf"] += 1
            sv = sg_[:, 0:a * b].re("p (a b) -> p a b", a=a)
            wv = wb_[:, 0:a * b].re("p (a b) -> p a b", a=a)
            k.dma(sv, src)
            k.copy("pool", wv, sv)
            return wv

        def load_w_to(dst, src, a, b, sg_=None):
            if sg_ is None:
                sg_ = stg[state.get("lane", 0) % 2]
            sv = sg_[:, 0:a * b].re("p (a b) -> p a b", a=a)
            wv = dst[:, 0:a * b].re("p (a b) -> p a b", a=a)
            k.dma(sv, src)
            k.copy("act", wv, sv)
            return wv

        def win_cols(l, c0):
            return I["mix_w_in"][l][:, c0:c0 + 128].rearrange("(kc p) m -> p kc m", p=128)

        k.dma(pcols, I["pcols"])
        k.dma(ident, Cn["ident"])
        for ph in range(2):
            k.dma(gmask[ph * 64:(ph + 1) * 64], Cn["gmask"].rearrange("a p n -> p a n"))
            k.dma(ntmask[ph * 64:(ph + 1) * 64], Cn["ntmask"].rearrange("a p n -> p a n"))
        k.dma(cmask, Cn["cmask"])
        k.dma(cmask_s, Cn["cmask_s"])
        k.dma(rmask, Cn["rmask"][:, 0:256])
        k.dma(rmask_s, Cn["rmask_s"])
        k.dma(rowmask, Cn["rowmask"])
        k.dma(blk1f, Cn["blk1"])
        k.dma(e4, Cn["e4"])
        k.copy("dve", blk1, blk1f)
        k.ts("dve", blk64, blk1f, 1.0 / 64.0, None, ALU.mult)
        k.memset("pool", onesD, 1.0 / 128.0)
        k.memset("pool", onesM, 1.0 / 1024.0)
        k.memset("pool", ones_row, 1.0)
        for i, v in enumerate([LN_EPS, GN_EPS, RMS_EPS, 1e-24, 1.0, 0.0]):
            k.memset("pool", epsc[:, i:i + 1], v)
        EPS_LN, EPS_GN, EPS_RMS, EPS_TINY = (epsc[:, i:i + 1] for i in range(4))
        k.ts("dve", pcA, pcols[:, 0:2 * L * 8], ALPHA, None, ALU.mult)
        k.memset("pool", lbc, 0.0)
        o_lb, _n = PCOL_OFF["hg_lb_logits"]
        k.tt("dve", lbc[:, 4:8], pcols[:, o_lb + 4:o_lb + 8], pcols[:, o_lb:o_lb + 4], ALU.subtract)
        k.act(lbc[:, 4:8], lbc[:, 4:8], AF.Sigmoid)
        k.ts("dve", omlc, lbc, -1.0, 1.0, ALU.mult, ALU.add)
        wa2f = ar([L, 256], name="wa2f")
        gw2f = ar([L, 256], name="gw2f")
        bsrow_f = stg[1][0:1, :]
        bss_f = ar([L * 4, 64], parts=1, name="bss_f")
        instage = stg
        colmask_f = stg[0].re("p (a b) -> p a b", a=16)
        k.dma(colmask_f, Cn["colmask"].rearrange("p (a b) -> p a b", a=16))
        k.copy("pool", colmask, colmask_f)
        for l in range(L):
            k.dma(wa2f[0:64, l, :], I["rw_w_w2"][l])
            k.dma(wa2f[64:128, l, :], I["rw_a_w2"][l])
            k.dma(gw2f[:, l, :], I["rw_g_w2"][l])
        k.copy("pool", wa2, wa2f)
        k.copy("pool", gw2, gw2f)
        k.dma(bsrow_f, I["cm_bs"].rearrange("l h t -> (l h t)").unsqueeze(0))
        k.copy("dve", bsrow, bsrow_f)
        for l in range(L if stop_after != "consts" else 0):
            for h in range(4 if stop_after != "cm1" else 1):
                li = l * 4 + h
                sg_ = stg[state["stg"] % 2]
                state["stg"] += 1
                k.dma(sg_[:, 0:128], I["cm_ws"][l, h])
                p_ = psum()
                k.tr(p_[:, 0:128], sg_[:, 0:128], ident)
                k.tt("dve", wct[:, li, :], p_[:, 0:128], cmask, ALU.mult)
                k.dma(sg_[0:4, 128:132], I["cm_ws"][l, h, 0:4, 0:4].rearrange("t s -> s t"), slow=True)
                p2 = psum()
                k.mm(p2[0:64, 0:4], [(e4[0:4, :], sg_[0:4, 128:132])])
                k.copy("act", sg_[0:64, 136:140], p2[0:64, 0:4])
                k.tt("dve", wcs[:, li, :].re("p (a b) -> p a b", a=16),
                     sg_[0:64, 136:140].re("p (o b) -> p o b", o=1).bc([64, 16, 4]),
                     cmask_s.re("p (a b) -> p a b", a=16), ALU.mult)
                k.dma(bss_f[0:1, li, :].re("p (a b) -> p a b", a=16),
                      I["cm_bs"][l, h:h + 1, 0:4].unsqueeze(1).broadcast_to([1, 16, 4]))
        k.copy("dve", bss, bss_f)

        import os
        _nb = int(os.environ.get("DBG_NB", "17"))
        _noact = os.environ.get("DBG_NOACT", "0") == "1"
        for bi in range(_nb if stop_after not in ("consts", "cm1", "cmall") else 0):
            sgi = instage[bi % 2]
            if bi < 16:
                nrow, ti, c0 = 128, bi // 4, (bi % 4) * 128
                k.dma(sgi, I["xp"][bi * 128:(bi + 1) * 128, :])
            else:
                nrow, ti, c0 = 64, 4, 0
                k.dma(sgi[0:64, :], I["xs"])
            for g in range(2):
                p_ = psum()
                pv = p_.re("p (a b) -> p a b", a=4)
                for q in range(4):
                    kc = g * 4 + q
                    k.tr(pv[:, q, 0:nrow], sgi[0:nrow, kc * 128:(kc + 1) * 128], ident[0:nrow, 0:nrow])
                _m = os.environ.get("DBG_MODE", "b")
                if _m == "a":
                    for q in range(4):
                        k.copy("act", x[ti][:, g * 4 + q, c0:c0 + nrow], pv[:, q, 0:nrow])
                    k.copy("pool", xb[ti][:, g * 4:(g + 1) * 4, c0:c0 + nrow], x[ti][:, g * 4:(g + 1) * 4, c0:c0 + nrow])
                elif _m == "b":
                    k.copy("dve", x[ti][:, g * 4:(g + 1) * 4, c0:c0 + nrow], pv[:, :, 0:nrow])
                    k.copy("act", xb[ti][:, g * 4:(g + 1) * 4, c0:c0 + nrow], x[ti][:, g * 4:(g + 1) * 4, c0:c0 + nrow])

        SCR = {}

        def alloc_ffn_ln(reset=True):
            if reset:
                arena_reset()
            SCR["usq"] = [ar([KC, 512], BF16, name=f"usq{i}") for i in range(2)]
            SCR["lnt"] = [[ar([512], name=f"lnt{i}_{j}") for j in range(4)] for i in range(2)]
            SCR["hbuf"] = [[ar([n], BF16, name=f"h{jj}_{ti}") for ti, (t0, n) in enumerate(TT)] for jj in range(6)]
            SCR["silu"] = [ar([512], name=f"silu_tmp{i}") for i in range(2)]

        def ln_lane(l, gname, bname, tiles, li, xscale=False):
            usq_all, lnt = SCR["usq"][li], SCR["lnt"][li]
            for ti in tiles:
                n = TT[ti][1]
                usq = usq_all[:, :, 0:n]
                k.act(xb[ti], x[ti], AF.Identity)
                k.act(usq, x[ti], AF.Square)
                yield
                pm = psum()
                k.mm(pm[:, 0:n], [(onesM, xb[ti][:, kc, :]) for kc in range(KC)])
                pq = psum()
                k.mm(pq[:, 0:n], [(onesM, usq[:, kc, :]) for kc in range(KC)])
                mean, var, rstd, nmr = (t_[:, 0:n] for t_ in lnt)
                k.copy("act", mean, pm[:, 0:n])
                k.tt("dve", var, mean, mean, ALU.mult)
                k.tt("dve", var, pq[:, 0:n], var, ALU.subtract)
                yield
                k.act(rstd, var, AF.Ln, bias=EPS_LN)
                k.act(rstd, rstd, AF.Exp, scale=-0.5)
                k.stt("dve", nmr, mean, -1.0, rstd, ALU.mult, ALU.mult)
                k.tt("dve", x[ti], x[ti], rstd.re("p (o n) -> p o n", o=1).bc([128, KC, n]), ALU.mult)
                k.tt("dve", x[ti], x[ti], nmr.re("p (o n) -> p o n", o=1).bc([128, KC, n]), ALU.add)
                yield
                for kc in range(KC):
                    k.act(xb[ti][:, kc, :], x[ti][:, kc, :], AF.Identity, bias=pc(bname, l, kc), scale=pc(gname, l, kc))
                for kc in range(KC):
                    if xscale:
                        gcol = pcA[:, l * 8 + kc:l * 8 + kc + 1]
                        bcol = pcA[:, L * 8 + l * 8 + kc:L * 8 + l * 8 + kc + 1]
                    else:
                        gcol, bcol = pc(gname, l, kc), pc(bname, l, kc)
                    k.ts("dve", x[ti][:, kc, :], x[ti][:, kc, :], gcol, bcol, ALU.mult, ALU.add)
                yield

        def layer_norm(l, gname, bname):
            xs_ = (gname == "ln1_g")
            run_lanes([ln_lane(l, gname, bname, (0, 2, 4), 0, xs_), ln_lane(l, gname, bname, (1, 3), 1, xs_)],
                      [[0, 1, 2, 3], [4, 5, 6, 7]])

        def ffn(l, wi, wo, gname, bname):
            hbuf, silu_tmp = SCR["hbuf"], SCR["silu"]
            _f = os.environ.get("DBG_FFN", "")
            for (j0, nj) in ROUNDS:
                for jj in range(nj):
                    j = j0 + jj
                    if _f in ("gu1",) and jj > 0:
                        raise _Stop()
                    wg = load_w(I[wi][l][:, j * 128:(j + 1) * 128].rearrange("(kc p) m -> p kc m", p=128), KC, 128)
                    wu = load_w(I[wi][l][:, DFF + j * 128:DFF + (j + 1) * 128].rearrange("(kc p) m -> p kc m", p=128), KC, 128)
                    for ti, (t0, n) in enumerate(TT):
                        pg = psum()
                        k.mm(pg[:, 0:n], [(wg[:, kc, :], xb[ti][:, kc, :]) for kc in range(KC)])
                        pu = psum()
                        k.mm(pu[:, 0:n], [(wu[:, kc, :], xb[ti][:, kc, :]) for kc in range(KC)])
                        state["eng"] += 1
                        stmp = silu_tmp[state["eng"] % 2][:, 0:n]
                        if _f == "gu1_nosilu":
                            raise _Stop()
                        k.act(stmp, pg[:, 0:n], AF.Silu)
                        if _f == "gu1_nomul":
                            raise _Stop()
                        k.stt("dve", hbuf[jj][ti], stmp, 0.5, pu[:, 0:n], ALU.mult, ALU.mult)
                if _f == "gu":
                    raise _Stop()
                for m in range(KC):
                    wo_ = load_w(I[wo][l][j0 * 128:(j0 + nj) * 128, m * 128:(m + 1) * 128].rearrange("(j p) m -> p j m", p=128), nj, 128)
                    for ti, (t0, n) in enumerate(TT):
                        po = psum()
                        k.mm(po[:, 0:n], [(wo_[:, jj, :], hbuf[jj][ti]) for jj in range(nj)])
                        if j0 == 0:
                            k.stt("dve", x[ti][:, m, :], x[ti][:, m, :], ALPHA, po[:, 0:n], ALU.mult, ALU.add)
                        else:
                            k.tt("dve", x[ti][:, m, :], x[ti][:, m, :], po[:, 0:n], ALU.add)
            if _f == "noln":
                raise _Stop()
            layer_norm(l, gname, bname)

        def contrib(outp, wo_, ti, off, n, first=False):
            for m in range(KC):
                po = psum()
                k.mm(po[:, 0:n], [(wo_[:, m, :], outp)])
                if first:
                    k.stt("dve", x[ti][:, m, off:off + n], x[ti][:, m, off:off + n], ALPHA, po[:, 0:n], ALU.mult, ALU.add)
                else:
                    k.tt("dve", x[ti][:, m, off:off + n], x[ti][:, m, off:off + n], po[:, 0:n], ALU.add)

        def project(wlist, ti, off, n, outs, funcs=None):
            for i, w_ in enumerate(wlist):
                p_ = psum()
                k.mm(p_[:, 0:n], [(w_[:, kc, :], xb[ti][:, kc, off:off + n]) for kc in range(KC)])
                f = None if funcs is None else funcs[i]
                if f is None:
                    k.copy("act" if i % 2 == 0 else "dve", outs[i][:, 0:n], p_[:, 0:n])
                else:
                    k.act(outs[i][:, 0:n], p_[:, 0:n], f)

        def head_stats(src, n, eps, blk, tmp_b, tmp_sq, mean, rstd, tmpf, one_bank=False):
            k.act(tmp_b[:, 0:n], src, AF.Identity)
            k.act(tmp_sq[:, 0:n], src, AF.Square)
            if one_bank:
                pb = psum()
                pm, pq = pb[:, 0:256], pb[:, 256:512]
            else:
                pm, pq = psum(), psum()
            k.mm(pm[:, 0:n], [(blk, tmp_b[:, 0:n])])
            k.mm(pq[:, 0:n], [(blk, tmp_sq[:, 0:n])])
            k.copy("act", mean, pm[:, 0:n])
            k.act(tmpf, mean, AF.Square)
            k.tt("dve", tmpf, pq[:, 0:n], tmpf, ALU.subtract)
            k.act(rstd, tmpf, AF.Ln, bias=eps)
            k.act(rstd, rstd, AF.Exp, scale=-0.5)

        def rwkv_alloc(sample):
            A = {}
            nb_ = 64 if sample else 256
            ncq = 1 if sample else 4
            A["W"] = wbf
            for nm in ["zr", "zk", "zv", "zwa", "zg", "dd", "sr", "sk", "sv", "swa", "sgg",
                       "logw", "aa", "oo", "t1", "t2"]:
                A[nm] = ar([nb_], name=nm)
            for nm in ["gg", "bsum"]:
                A[nm] = ar([nb_], BF16, name=nm)
            A["mean"], A["rstd"], A["En"], A["Epm"], A["Et"] = A["logw"], A["aa"], A["swa"], A["sgg"], A["dd"]
            for nm in ["twx", "sgb", "sqb", "obb", "osq", "outp"]:
                A[nm] = ar([nb_], BF16, name=nm)
            A["AR"] = ar([ncq, 128], BF16, name="AR")
            A["BK"] = ar([ncq, 2, 64], BF16, name="BK")
            A["BKd"] = ar([ncq, 2, 64], name="BKd")
            A["Vt"] = [ar([128], BF16, parts=64, name=f"Vt{c}") for c in range(ncq)]
            A["Bdt"] = [ar([128], BF16, parts=64, name=f"Bdt{c}") for c in range(ncq)]
            A["Kdt"] = [ar([128], BF16, parts=64, name=f"Kdt{c}") for c in range(ncq)]
            A["zlast"] = ar([8], name="zlast")
            A["S32"] = ar([128], name="S32")
            A["Sbf"] = ar([128], BF16, name="Sbf")
            A["sst"] = ar([4, 128], name="sst")
            if sample:
                A["S32s"] = ar([16, 128], name="S32s")
                A["Sbfs"] = ar([16, 128], BF16, name="Sbfs")
                A["ARm"] = ar([2, 16, 64], BF16, name="ARm")
                A["sh0"] = ar([5, 16], name="sh0")
                A["Uwm"] = ar([16, 64], BF16, parts=64, name="Uwm")
                A["Vtm"] = ar([16, 64], BF16, parts=64, name="Vtm")
            A["chain"] = []
            for i in range(ncq):
                u = {}
                for nm in ["P0", "P1", "PT0", "PT1", "T0", "T1"]:
                    u[nm] = ar([128], BF16, name=nm + str(i))
                k.memset("pool", u["P0"], 0.0)
                k.memset("pool", u["PT0"], 0.0)
                A["chain"].append(u)
            A["ures"] = {}
            A["Tbd"] = [ar([128], BF16, name=f"Tbd{c}") for c in range(ncq)]
            for c in range(ncq):
                for h2 in range(2):
                    A["ures"][(c, h2)] = {"GAb": ar([192], BF16, parts=64, name=f"GAb{c}{h2}")}
            A["Xbf"] = ar([64], BF16, name="Xbf")
            A["Uw"] = ar([128], BF16, parts=64, name="Uw")
            return A

        def rwkv_stageA(A, chunks, sample):
            mi = 1 if sample else 0
            AR, BK = A["AR"], A["BK"]
            nl = 1 if sample else 5
            h0, h1 = slice(0, 64), slice(64, 128)
            st_ = [{"c": c, "t": A["chain"][c]} for c in chunks]
            for g0 in range(0, len(st_), 2):
                grp = st_[g0:g0 + 2]
                yield
                for u in grp:
                    c = u["c"]
                    u["pG0"], u["pG1"] = psum(), psum()
                    k.mm(u["pG0"][0:64, 0:128], [(BK[h0, c, 0, :], AR[h0, c, :])])
                    k.mm(u["pG1"][0:64, 0:128], [(BK[h1, c, 0, :], AR[h1, c, :])])
                    k.mm(u["pG0"][0:64, 128:256], [(BK[h0, c, 1, :], AR[h0, c, :])])
                    k.mm(u["pG1"][0:64, 128:256], [(BK[h1, c, 1, :], AR[h1, c, :])])
                yield
                for u in grp:
                    c = u["c"]
                    k.tt("dve", u["t"]["P0"][h0, 0:64], u["pG0"][0:64, 0:64], gmask[h0, mi, 0:64], ALU.mult)
                    k.tt("dve", A["ures"][(c, 0)]["GAb"], u["pG0"][0:64, 64:256], gmask[h0, mi, 64:256], ALU.mult)
                    k.tt("dve", A["ures"][(c, 1)]["GAb"], u["pG1"][0:64, 64:256], gmask[h0, mi, 64:256], ALU.mult)
            for g0 in range(0, len(st_), 2):
                grp = st_[g0:g0 + 2]
                yield
                for u in grp:
                    c = u["c"]
                    u["pN0"], u["pN1"] = psum(), psum()
                    k.mm(u["pN0"][0:64, 0:64], [(AR[h0, c, 0:64], BK[h0, c, 0, :])])
                    k.mm(u["pN1"][64:128, 0:64], [(AR[h1, c, 0:64], BK[h1, c, 0, :])], tp=(64, 64))
                    k.mm(u["pN1"][64:128, 64:128], [(BK[h1, c, 0, :], AR[h1, c, 0:64])], tp=(64, 64))
                yield
                for u in grp:
                    t = u["t"]
                    k.tt("dve", t["PT0"][h0, 0:64], u["pN0"][0:64, 0:64], ntmask[h0, mi, :], ALU.mult)
                    k.tt("dve", t["PT0"][h1, 64:128], u["pN1"][64:128, 0:64], ntmask[h1, mi, :], ALU.mult)
                    k.tt("dve", t["P0"][h1, 64:128], u["pN1"][64:128, 64:128], gmask[h1, mi, 0:64], ALU.mult)
                    k.tt("dve", t["T0"], t["P0"], ident, ALU.add)
                    u["P"], u["PT"], u["Tm"] = t["P0"], t["PT0"], t["T0"]
            for lv in range(1, nl + 1):
                yield
                for u in st_:
                    t = u["t"]
                    u["Pn"] = t["P1"] if u["P"] is t["P0"] else t["P0"]
                    u["PTn"] = t["PT1"] if u["PT"] is t["PT0"] else t["PT0"]
                    u["Tn"] = t["T1"] if u["Tm"] is t["T0"] else t["T0"]
                    u["p12"] = psum()
                    k.mm(u["p12"][:, 0:128], [(u["P"], u["PT"])])
                    if lv < nl:
                        k.mm(u["p12"][:, 128:256], [(u["PT"], u["P"])])
                yield
                for j, u in enumerate(st_):
                    ev = "act" if j % 2 == 0 else "dve"
                    k.copy(ev, u["PTn"], u["p12"][:, 0:128])
                    if lv < nl:
                        k.copy(ev, u["Pn"], u["p12"][:, 128:256])
                yield
                for u in st_:
                    u["p3"] = psum()
                    k.mm(u["p3"][:, 0:128], [(u["PTn"], u["Tm"])])
                yield
                for u in st_:
                    if lv < nl:
                        k.tt("dve", u["Tn"], u["Tm"], u["p3"][:, 0:128], ALU.add)
                    else:
                        k.tt("dve", A["Tbd"][u["c"]], u["Tm"], u["p3"][:, 0:128], ALU.add)
                    u["P"], u["PT"], u["Tm"] = u["Pn"], u["PTn"], u["Tn"]

        def rwkv_stageB(A, c, sample):
            AR = A["AR"]
            Vt = A["Vt"][c]
            H = []
            for h2 in range(2):
                hr = slice(h2 * 64, h2 * 64 + 64)
                res = A["ures"][(c, h2)]
                GAb = res["GAb"]
                H.append({"hr": hr, "h2": h2, "res": res, "A_rb": GAb[:, 0:64], "A_ak": GAb[:, 64:128], "A_rk": GAb[:, 128:192],
                          "Xbf": A["Xbf"][hr, :], "T": A["Tbd"][c][hr, hr]})
            yield
            for u in H:
                hc = u["hr"]
                u["pX"] = psum()
                if sample:
                    pairs = [(A["ARm"][:, 0, b, :], A["Sbfs"][:, b, hc]) for b in range(16)]
                else:
                    pairs = [(AR[:, c, 0:64], A["Sbf"][:, hc])]
                pairs.append((u["A_ak"], Vt[:, hc]))
                if u["h2"] == 0:
                    k.mm(u["pX"][0:64, 0:64], pairs)
                else:
                    k.mm(u["pX"][64:128, 0:64], pairs, tp=(0, 64))
            yield
            for u in H:
                k.copy("act", u["Xbf"], u["pX"][u["hr"], 0:64])
            yield
            for u in H:
                u["pU"] = psum()
                k.mm(u["pU"][0:64, 0:64], [(u["T"], u["Xbf"])])
            yield
            for u in H:
                k.copy("act", A["Uw"][:, u["hr"]], u["pU"][0:64, 0:64])
            yield
            for u in H:
                u["pO"] = psum()
                if sample:
                    pairs = [(A["Sbfs"][:, b, :], A["ARm"][:, 1, b, :]) for b in range(16)]
                else:
                    pairs = [(A["Sbf"], AR[:, c, 64:128])]
                pairs += [(A["Uw"], u["A_rb"]), (Vt, u["A_rk"])]
                k.mm(u["pO"][:, 0:64], pairs)
            if not sample:
                yield
                for u in H:
                    u["pS"] = psum()
                    k.mm(u["pS"][:, 0:64], [(A["Bdt"][c], A["Uw"][:, u["hr"]]), (A["Kdt"][c], Vt[:, u["hr"]])])
                yield
                for u in H:
                    hr = u["hr"]
                    k.stt("dve", A["S32"][hr, hr], A["S32"][hr, hr], A["Ep"][hr, c * 64 + 63:c * 64 + 64], u["pS"][hr, 0:64],
                          ALU.mult, ALU.add)
                yield
                for u in H:
                    hr = u["hr"]
                    k.copy("act", A["oo"][hr, c * 64:(c + 1) * 64], u["pO"][hr, 0:64])
                yield
                for u in H:
                    hr = u["hr"]
                    k.copy("act", A["Sbf"][hr, hr], A["S32"][hr, hr])
            else:
                yield
                for u in H:
                    hr = u["hr"]
                    k.copy("act", A["oo"][hr, c * 64:(c + 1) * 64], u["pO"][hr, 0:64])
                yield
                for u in H:
                    hr = u["hr"]
                    hc = hr
                    k.tt("dve", A["Uwm"], A["Uw"][:, hc].re("p (o v) -> p o v", o=1).bc([64, 16, 64]),
                         rowmask.re("p (b o) -> p b o", o=1).bc([64, 16, 64]), ALU.mult)
                    k.tt("dve", A["Vtm"], Vt[:, hc].re("p (o v) -> p o v", o=1).bc([64, 16, 64]),
                         rowmask.re("p (b o) -> p b o", o=1).bc([64, 16, 64]), ALU.mult)
                    Eps = A["Ep"][:, 0:64].re("p (b t) -> p b t", t=4)
                    for g in range(2):
                        pS = psum()
                        for b8 in range(8):
                            b = g * 8 + b8
                            k.mm(pS[:, b8 * 64:(b8 + 1) * 64], [(A["Bdt"][0], A["Uwm"][:, b, :]), (A["Kdt"][0], A["Vtm"][:, b, :])])
                        bs_ = slice(g * 8, g * 8 + 8)
                        k.tt("dve", A["S32s"][hr, bs_, hc], A["S32s"][hr, bs_, hc], Eps[hr, bs_, 3:4].bc([64, 8, 64]), ALU.mult)
                        k.tt("dve", A["S32s"][hr, bs_, hc], A["S32s"][hr, bs_, hc],
                             pS[hr, :].re("p (b v) -> p b v", b=8), ALU.add)

        def rwkv_part(A, l, hp, subtiles):
            Wr = load_w_to(A["W"][0], win_cols(l, hp * 128), KC, 128)
            Wk = load_w_to(A["W"][1], win_cols(l, 256 + hp * 128), KC, 128)
            Wv = load_w_to(A["W"][2], win_cols(l, 512 + hp * 128), KC, 128)
            Wwa = load_w_to(A["W"][3], win_cols(l, 768), KC, 128)
            Wg = load_w_to(A["W"][4], win_cols(l, 896), KC, 128)
            Wo = load_w_to(A["W"][5], I["mix_w_out"][l][hp * 128:(hp + 1) * 128, :].rearrange("p (a b) -> p a b", a=KC), KC, 128)
            yield
            zchunk = [hp, 2 + hp, 4 + hp, 6, 7]
            zs = [A["zr"], A["zk"], A["zv"], A["zwa"], A["zg"]]
            ss = [A["sr"], A["sk"], A["sv"], A["swa"], A["sgg"]]
            k.memset("pool", A["S32"], 0.0)
            k.memset("pool", A["Sbf"], 0.0)
            k.memset("pool", A["Uw"], 0.0)
            k.memset("pool", A["zlast"], 0.0)
            k.memset("pool", A["sst"], 0.0)
            for (ti, off, n) in subtiles:
                sample = ti == 4
                lastp = (ti == 3 and off == 256)
                nch = n // 64
                project([Wr, Wk, Wv, Wwa, Wg], ti, off, n, zs)
                _r = os.environ.get("DBG_RW", "")
                yield
                if _r == "proj":
                    raise _Stop()
                if sample:
                    for i, zc in enumerate(zchunk):
                        k.dma(A["sh0"][:, i, :], I["st_sh"][l][:, zc * 128:(zc + 1) * 128].rearrange("b p -> p b"), slow=True)
                    for g in range(4):
                        for h2 in range(2):
                            hr = slice(h2 * 64, h2 * 64 + 64)
                            k.dma(A["sst"][hr, :, hr], I["st_rw"][l][g * 4:(g + 1) * 4, 2 * hp + h2, :, :].rearrange("b v k -> v b k"))
                        p_ = psum()
                        pv = p_.re("p (a b) -> p a b", a=4)
                        for q in range(4):
                            k.tr(pv[:, q, :], A["sst"][:, q, :], ident)
                        k.copy("dve", A["S32s"][:, g * 4:(g + 1) * 4, :], pv)
                        k.copy("act", A["Sbfs"][:, g * 4:(g + 1) * 4, :], A["S32s"][:, g * 4:(g + 1) * 4, :])
                for i in range(5):
                    z, s_, d = zs[i], ss[i], A["dd"]
                    mu = pc("rw_mu", l, zchunk[i])
                    if not sample:
                        k.tt("dve", d[:, 1:n], z[:, 0:n - 1], z[:, 1:n], ALU.subtract)
                        k.tt("pool", d[:, 0:1], A["zlast"][:, i:i + 1], z[:, 0:1], ALU.subtract)
                        k.copy("pool", A["zlast"][:, i:i + 1], z[:, n - 1:n])
                    else:
                        z3 = z[:, 0:64].re("p (b t) -> p b t", t=4)
                        d3 = d[:, 0:64].re("p (b t) -> p b t", t=4)
                        k.tt("dve", d3[:, :, 1:4], z3[:, :, 0:3], z3[:, :, 1:4], ALU.subtract)
                        k.tt("pool", d3[:, :, 0:1], A["sh0"][:, i, :].re("p (b o) -> p b o", o=1), z3[:, :, 0:1], ALU.subtract)
                    k.stt("dve", s_[:, 0:n], d[:, 0:n], mu, z[:, 0:n], ALU.mult, ALU.add)
                yield
                if _r == "shift":
                    raise _Stop()
                if lastp or sample:
                    for i, zc in enumerate(zchunk):
                        if i >= 3 and hp == 1:
                            continue
                        if sample:
                            k.dma(O["rw_sh_s"][l][:, zc * 128:(zc + 1) * 128].rearrange("b p -> p b"),
                                  zs[i][:, 0:64].re("p (b t) -> p b t", t=4)[:, :, 3], slow=True)
                        else:
                            k.dma(O["rw_sh_p"][l, zc * 128:(zc + 1) * 128].rearrange("(p o) -> p o", o=1),
                                  zs[i][:, n - 1:n], slow=True)
                sr, sk, sv, swa, sgg = (t_[:, 0:n] for t_ in ss)
                twx, sgb, sqb = A["twx"][:, 0:n], A["sgb"][:, 0:n], A["sqb"][:, 0:n]
                k.act(twx[0:64], swa[0:64], AF.Tanh)
                k.copy("pool", twx[64:128], swa[64:128])
                k.act(sgb, sgg, AF.Sigmoid)
                pW = psum()
                k.mm(pW[:, 0:n], [(wa2[0:64, l, hp * 128:(hp + 1) * 128], twx[0:64])])
                pA = psum()
                k.mm(pA[:, 0:n], [(wa2[64:128, l, hp * 128:(hp + 1) * 128], twx[64:128])])
                pGt = psum()
                k.mm(pGt[:, 0:n], [(gw2[:, l, hp * 128:(hp + 1) * 128], sgb)])
                logw, aa, gg = A["logw"][:, 0:n], A["aa"][:, 0:n], A["gg"][:, 0:n]
                k.act(logw, pW[:, 0:n], AF.Sigmoid, bias=pc("rw_w0", l, hp))
                k.act(logw, logw, AF.Identity, scale=WSCALE)
                k.act(aa, pA[:, 0:n], AF.Sigmoid, bias=pc("rw_a0", l, hp))
                k.copy("dve", gg, pGt[:, 0:n])
                yield
                if _r == "lora":
                    raise _Stop()
                kk, kp, bb, cum, Ep = (A[nm][:, 0:n] for nm in ["zr", "zk", "zv", "zwa", "zg"])
                A["Ep"] = A["zg"]
                t1, t2 = A["t1"][:, 0:n], A["t2"][:, 0:n]
                k.act(kk, sk, AF.Identity, scale=pc("rw_k_k", l, hp))
                k.act(sqb, kk, AF.Square)
                pN = psum()
                k.mm(pN[:, 0:n], [(blk1, sqb)])
                k.act(t1, pN[:, 0:n], AF.Ln, bias=EPS_TINY)
                k.act(t1, t1, AF.Exp, scale=-0.5)
                k.tt("dve", kk, kk, t1, ALU.mult)
                k.ts("pool", t2, aa, -1.0, pc("rw_k_a", l, hp), ALU.add, ALU.mult)
                k.stt("dve", kp, t2, 1.0, sk, ALU.add, ALU.mult)
                k.tt("dve", bb, kk, aa, ALU.mult)
                k.stt("dve", sqb, sr, pc("rw_r_k", l, hp), kp, ALU.mult, ALU.mult)
                pB = psum()
                k.mm(pB[:, 0:n], [(blk1, sqb)])
                k.copy("act", A["bsum"][:, 0:n], pB[:, 0:n])
                yield
                if _r == "kk":
                    raise _Stop()
                k.scan(cum, (rmask_s if sample else rmask)[:, 0:n], logw)
                En, Epm, Et = A["En"][:, 0:n], A["Epm"][:, 0:n], A["Et"][:, 0:n]
                k.act(Ep, cum, AF.Exp)
                k.act(En, cum, AF.Exp, scale=-1.0)
                k.tt("dve", Epm, cum, logw, ALU.subtract)
                k.act(Epm, Epm, AF.Exp)
                cl = 4 if sample else 64
                cum3 = cum.re("p (c t) -> p c t", t=cl)
                k.tt("pool", Et.re("p (c t) -> p c t", t=cl), cum3[:, :, cl - 1:cl].bc([128, n // cl, cl]), cum3, ALU.subtract)
                k.act(Et, Et, AF.Exp)
                AR, BK, BKd = A["AR"], A["BK"], A["BKd"]
                v3 = lambda t_: t_.re("p (c t) -> p c t", t=64)
                k.stt("dve", AR[:, 0:nch, 0:64], v3(kk), -1.0, v3(Epm), ALU.mult, ALU.mult)
                k.tt("pool", AR[:, 0:nch, 64:128], v3(sr), v3(Ep), ALU.mult)
                k.tt("dve", BK[:, 0:nch, 0, :], v3(bb), v3(En), ALU.mult)
                k.tt("pool", BK[:, 0:nch, 1, :], v3(kp), v3(En), ALU.mult)
                k.tt("dve", BKd[:, 0:nch, 0, :], v3(bb), v3(Et), ALU.mult)
                k.tt("pool", BKd[:, 0:nch, 1, :], v3(kp), v3(Et), ALU.mult)
                if sample:
                    k.tt("dve", A["ARm"][:, 0, :, :], AR[:, 0, 0:64].re("p (o t) -> p o t", o=1).bc([128, 16, 64]), colmask, ALU.mult)
                    k.tt("dve", A["ARm"][:, 1, :, :], AR[:, 0, 64:128].re("p (o t) -> p o t", o=1).bc([128, 16, 64]), colmask, ALU.mult)
                yield
                if _r == "prep":
                    raise _Stop()
                for c in range(nch):
                    p_ = psum()
                    k.tr(p_[0:64, 0:128], sv[:, c * 64:(c + 1) * 64], ident)
                    k.tr(p_[0:64, 128:256], BKd[:, c, 0, :], ident)
                    k.tr(p_[0:64, 256:384], BKd[:, c, 1, :], ident)
                    ev = "act" if c % 2 == 0 else "dve"
                    k.copy(ev, A["Vt"][c], p_[0:64, 0:128])
                    k.copy(ev, A["Bdt"][c], p_[0:64, 128:256])
                    k.copy(ev, A["Kdt"][c], p_[0:64, 256:384])
                yield
                if _r == "tr":
                    raise _Stop()
                yield from rwkv_stageA(A, list(range(nch)), sample)
                for c in range(nch):
                    yield from rwkv_stageB(A, c, sample)
                yield
                if _r == "units":
                    raise _Stop()
                oo = A["oo"][:, 0:n]
                mean, rstd = A["mean"][:, 0:n], A["rstd"][:, 0:n]
                head_stats(oo, n, EPS_GN, blk64, A["obb"], A["osq"], mean, rstd, t1)
                k.tt("pool", t2, oo, mean, ALU.subtract)
                k.tt("dve", t2, t2, rstd, ALU.mult)
                k.act(t2, t2, AF.Identity, bias=pc("rw_gn_b", l, hp), scale=pc("rw_gn_g", l, hp))
                k.tt("pool", t1, A["bsum"][:, 0:n], sv, ALU.mult)
                k.tt("dve", t2, t2, t1, ALU.add)
                outp = A["outp"][:, 0:n]
                k.tt("dve", outp, t2, gg, ALU.mult)
                if l == 0 and hp == 0 and ti == 0 and off == 0:
                    tap("rw_o", A["oo"])
                    if "rw_out" in TAPO:
                        k.tt("pool", t1, t2, gg, ALU.mult)
                        tap("rw_out", A["t1"])
                yield
                if _r == "gn":
                    raise _Stop()
                contrib(outp, Wo, ti, off, n)
                yield
                if _r == "tile1":
                    raise _Stop()
                if lastp:
                    p_ = psum()
                    k.tr(p_[:, 0:128], A["S32"], ident)
                    k.copy("act", A["sst"][:, 0, :], p_[:, 0:128])
                    for h2 in range(2):
                        hr = slice(h2 * 64, h2 * 64 + 64)
                        k.dma(O["rw_S_p"][l, 2 * hp + h2], A["sst"][hr, 0, hr])
                if sample:
                    for g in range(4):
                        p_ = psum()
                        pv = p_.re("p (a b) -> p a b", a=4)
                        for q in range(4):
                            k.tr(pv[:, q, :], A["S32s"][:, g * 4 + q, :], ident)
                        k.copy("dve", A["sst"], pv)
                        for h2 in range(2):
                            hr = slice(h2 * 64, h2 * 64 + 64)
                            k.dma(O["rw_S_s"][l][g * 4:(g + 1) * 4, 2 * hp + h2, :, :].rearrange("b v k -> v b k"), A["sst"][hr, :, hr])

        def hgrn_alloc(sample):
            A = {}
            nb_ = 64 if sample else 256
            ncq = 1 if sample else 4
            A["W"] = [ar([1024], BF16, name=f"hgW{i}") for i in range(5)]
            for nm in ["qs", "ff", "ii", "logf", "kf", "cum", "Ep", "En", "Et", "kd", "oo"]:
                A[nm] = ar([nb_], name=nm)
            A["t1"] = A["logf"]
            for nm in ["qt", "kt", "osq", "outp", "sog"]:
                A[nm] = ar([nb_], BF16, name=nm)
            A["It"] = [ar([128], BF16, parts=64, name=f"It{c}") for c in range(ncq)]
            A["Kdt"] = [ar([128], BF16, parts=64, name=f"Kdt{c}") for c in range(ncq)]
            A["Am"] = ar([64], BF16, parts=64, name="Am")
            A["S32"] = ar([128], name="S32")
            A["Sbf"] = ar([128], BF16, name="Sbf")
            if sample:
                A["S32s"] = ar([16, 128], name="S32s")
                A["Sbfs"] = ar([16, 128], BF16, name="Sbfs")
                A["qm"] = ar([16, 64], BF16, name="qm")
                A["Itm"] = ar([16, 128], BF16, parts=64, name="Itm")
            return A

        def hgrn_part(A, l, h, subtiles):
            base = 1024 + h * 128
            Wq = load_w_to(A["W"][0], win_cols(l, base), KC, 128)
            Wf = load_w_to(A["W"][1], win_cols(l, base + 512), KC, 128)
            Wi = load_w_to(A["W"][2], win_cols(l, base + 1024), KC, 128)
            Wog = load_w_to(A["W"][3], win_cols(l, base + 1536), KC, 128)
            Wo = load_w_to(A["W"][4], I["mix_w_out"][l][256 + h * 128:256 + (h + 1) * 128, :].rearrange("p (a b) -> p a b", a=KC), KC, 128)
            yield
            k.memset("pool", A["S32"], 0.0)
            k.memset("pool", A["Sbf"], 0.0)
            lb = lbc[:, l * 4 + h:l * 4 + h + 1]
            oml = omlc[:, l * 4 + h:l * 4 + h + 1]
            for (ti, off, n) in subtiles:
                sample = ti == 4
                lastp = (ti == 3 and off == 256)
                nch = n // 64
                project([Wq, Wf, Wi, Wog], ti, off, n, [A["qs"], A["ff"], A["ii"], A["sog"]],
                        [AF.Silu, AF.Sigmoid, None, AF.Silu])
                yield
                qs, ff, ii, sog = (A[nm][:, 0:n] for nm in ["qs", "ff", "ii", "sog"])
                logf, kf, cum, Ep, En, Et, kd, oo, t1 = (A[nm][:, 0:n] for nm in
                                                          ["logf", "kf", "cum", "Ep", "En", "Et", "kd", "oo", "t1"])
                qt, kt = A["qt"][:, 0:n], A["kt"][:, 0:n]
                if sample:
                    k.dma(A["S32s"], I["st_hg"][l][:, h, :, :].rearrange("b k v -> k b v"))
                    k.copy("act", A["Sbfs"], A["S32s"])
                k.ts("dve", ff, ff, oml, lb, ALU.mult, ALU.add)
                k.ts("pool", ff, ff, 1e-30, None, ALU.max)
                k.act(logf, ff, AF.Ln)
                k.act(kf, ff, AF.Identity, bias=epsc[:, 4:5], scale=-1.0)
                k.scan(cum, (rmask_s if sample else rmask)[:, 0:n], logf)
                k.act(Ep, cum, AF.Exp)
                k.act(En, cum, AF.Exp, scale=-1.0)
                cl = 4 if sample else 64
                cum3 = cum.re("p (c t) -> p c t", t=cl)
                k.tt("pool", Et.re("p (c t) -> p c t", t=cl), cum3[:, :, cl - 1:cl].bc([128, n // cl, cl]), cum3, ALU.subtract)
                k.act(Et, Et, AF.Exp)
                k.tt("dve", qt, qs, Ep, ALU.mult)
                k.tt("pool", kt, kf, En, ALU.mult)
                k.tt("dve", kd, kf, Et, ALU.mult)
                yield
                if sample:
                    k.tt("dve", A["qm"], qt.re("p (o t) -> p o t", o=1).bc([128, 16, 64]), colmask, ALU.mult)
                for c in range(nch):
                    cs = slice(c * 64, (c + 1) * 64)
                    p_ = psum()
                    k.tr(p_[0:64, 0:128], ii[:, cs], ident)
                    k.tr(p_[0:64, 128:256], kd[:, cs], ident)
                    ev = "act" if c % 2 == 0 else "dve"
                    k.copy(ev, A["It"][c], p_[0:64, 0:128])
                    k.copy(ev, A["Kdt"][c], p_[0:64, 128:256])
                mi = 1 if sample else 0
                yield
                for c in range(nch):
                    yield
                    cs = slice(c * 64, (c + 1) * 64)
                    pA = psum()
                    k.mm(pA[0:64, 0:64], [(kt[:, cs], qt[:, cs])])
                    k.tt("dve", A["Am"], pA[0:64, 0:64], gmask[0:64, mi, 64:128], ALU.mult)
                    pO = psum()
                    if sample:
                        pairs = [(A["Sbfs"][:, b, :], A["qm"][:, b, :]) for b in range(16)]
                    else:
                        pairs = [(A["Sbf"], qt[:, cs])]
                    pairs.append((A["It"][c], A["Am"]))
                    k.mm(pO[:, 0:64], pairs)
                    k.copy("act", oo[:, cs], pO[:, 0:64])
                    if not sample:
                        pS = psum()
                        k.mm(pS[:, 0:128], [(A["Kdt"][c], A["It"][c])])
                        k.stt("dve", A["S32"], A["S32"], Ep[:, c * 64 + 63:c * 64 + 64], pS[:, 0:128], ALU.mult, ALU.add)
                        k.copy("act", A["Sbf"], A["S32"])
                    else:
                        k.tt("dve", A["Itm"], A["It"][0].re("p (o v) -> p o v", o=1).bc([64, 16, 128]),
                             rowmask.re("p (b o) -> p b o", o=1).bc([64, 16, 128]), ALU.mult)
                        Eps = Ep.re("p (b t) -> p b t", t=4)
                        for g in range(4):
                            pS = psum()
                            for b4 in range(4):
                                b = g * 4 + b4
                                k.mm(pS[:, b4 * 128:(b4 + 1) * 128], [(A["Kdt"][0], A["Itm"][:, b, :])])
                            bs_ = slice(g * 4, g * 4 + 4)
                            k.tt("dve", A["S32s"][:, bs_, :], A["S32s"][:, bs_, :], Eps[:, bs_, 3:4].bc([128, 4, 128]), ALU.mult)
                            k.tt("dve", A["S32s"][:, bs_, :], A["S32s"][:, bs_, :], pS.re("p (b v) -> p b v", b=4), ALU.add)
                yield
                osq = A["osq"][:, 0:n]
                k.act(osq, oo, AF.Square)
                pq = psum()
                k.mm(pq[:, 0:n], [(onesD, osq)])
                k.act(t1, pq[:, 0:n], AF.Ln, bias=EPS_RMS)
                k.act(t1, t1, AF.Exp, scale=-0.5)
                k.tt("dve", t1, t1, oo, ALU.mult)
                outp = A["outp"][:, 0:n]
                k.stt("dve", outp, t1, pc("hg_norm_g", l, h), sog, ALU.mult, ALU.mult)
                yield
                contrib(outp, Wo, ti, off, n)
                yield
                if lastp:
                    k.dma(O["hg_S_p"][l, h], A["S32"])
                if sample:
                    k.dma(O["hg_S_s"][l][:, h, :, :].rearrange("b k v -> k b v"), A["S32s"])

        def cm_alloc(sample):
            A = {}
            nb_ = 64 if sample else 256
            A["W"] = [ar([1024], BF16, name=f"cmW{i}") for i in range(3)]
            if sample:
                A["stg"] = stg[1]
                for nm in ["uu", "vv", "mean", "rstd"]:
                    A[nm] = ar([nb_], name=nm)
            else:
                A["stg"] = ar([1024], name="cmstg")
                for i_, nm in enumerate(["uu", "vv", "mean", "rstd"]):
                    A[nm] = T(A["stg"].ap[:, i_ * 256:(i_ + 1) * 256], Buf(nm))
            for nm in ["vb", "vsq", "outp"]:
                A[nm] = ar([nb_], BF16, name=nm)
            A["Vt"] = ar([128], BF16, name="Vt")
            A["Vt32"] = ar([128], name="Vt32", parts=64)
            return A

        def cm_part(A, l, hp, subtiles):
            Wu = load_w_to(A["W"][0], win_cols(l, 3072 + hp * 128), KC, 128, A["stg"])
            Wv = load_w_to(A["W"][1], win_cols(l, 3328 + hp * 128), KC, 128, A["stg"])
            Wo = load_w_to(A["W"][2], I["mix_w_out"][l][768 + hp * 128:768 + (hp + 1) * 128, :].rearrange("p (a b) -> p a b", a=KC), KC, 128, A["stg"])
            yield
            for (ti, off, n) in subtiles:
                sample = ti == 4
                project([Wu, Wv], ti, off, n, [A["uu"], A["vv"]], [AF.Gelu, AF.Gelu])
                yield
                uu, vv, mean, rstd = (A[nm][:, 0:n] for nm in ["uu", "vv", "mean", "rstd"])
                head_stats(vv, n, EPS_LN, blk64, A["vb"], A["vsq"], mean, rstd, rstd, one_bank=sample)
                k.tt("pool", vv, vv, mean, ALU.subtract)
                k.tt("dve", vv, vv, rstd, ALU.mult)
                k.act(vv, vv, AF.Identity, bias=pc("cm_ln_b", l, hp), scale=pc("cm_ln_g", l, hp))
                outp = A["outp"][:, 0:n]
                yield
                nb = 1 if sample else 2
                bl = 64 if sample else 128
                for cb in range(nb):
                    cs = slice(cb * bl, (cb + 1) * bl)
                    p_ = psum()
                    k.tr(p_[0:bl, 0:128], vv[:, cs], ident)
                    k.copy("act", A["Vt"][0:bl, :], p_[0:bl, 0:128])
                    if sample:
                        k.copy("act", A["Vt32"], p_[0:64, 0:128])
                        k.dma(O["cm_v_s"][l][:, :, hp * 128:(hp + 1) * 128].rearrange("b t d -> (b t) d"), A["Vt32"])
                    for h2 in range(2):
                        hr = slice(h2 * 64, h2 * 64 + 64)
                        li = l * 4 + 2 * hp + h2
                        pM = psum()
                        if sample:
                            k.mm(pM[:, 0:64], [(A["Vt"][0:64, :], wcs[:, li, :]), (ones_row[0:1, :], bss[0:1, li, :])])
                        else:
                            k.mm(pM[:, 0:128], [(A["Vt"], wct[:, li, :]),
                                                (ones_row[0:1, :], bsrow[0:1, li * 128:(li + 1) * 128])])
                        k.tt("dve", outp[hr, cs], uu[hr, cs], pM[hr, 0:bl], ALU.mult)
                yield
                contrib(outp, Wo, ti, off, n)
                yield

        def chk(name):
            if stop_after == name:
                raise _Stop()

        def run_lanes(gens, banks):
            state["banks"] = banks
            lists = []
            try:
                for i, g in enumerate(gens):
                    state["lane"] = i
                    S.defer = []
                    for _ in g:
                        pass
                    lists.append(S.defer)
            finally:
                S.defer = None
                state["banks"] = None
                state["lane"] = 0
            tot = [len(x) for x in lists]
            pos = [0] * len(lists)
            eng_free = {e: 0.0 for e in ALLENG}
            bw, br = {}, {}
            HOP = 0.10
            while True:
                best, bi, bstart = None, -1, 0.0
                for i in range(len(lists)):
                    if pos[i] >= tot[i]:
                        continue
                    eng, fn, rd, wr, dma, cost, single = lists[i][pos[i]]
                    st_t = eng_free[eng]
                    for b_ in rd:
                        st_t = max(st_t, bw.get(id(b_), 0.0) + HOP)
                    for b_ in wr:
                        st_t = max(st_t, bw.get(id(b_), 0.0) + HOP, br.get(id(b_), 0.0) + HOP)
                    key = (round(st_t, 1), pos[i] / tot[i])
                    if best is None or key < best:
                        best, bi, bstart = key, i, st_t
                if bi < 0:
                    break
                eng, fn, rd, wr, dma, cost, single = lists[bi][pos[bi]]
                fin = bstart + cost
                eng_free[eng] = bstart + (0.1 if dma else cost)
                for b_ in rd:
                    br[id(b_)] = max(br.get(id(b_), 0.0), fin)
                for b_ in wr:
                    bw[id(b_)] = fin
                    br[id(b_)] = 0.0
                S.op(eng, fn, rd, wr, dma, cost, single)
                pos[bi] += 1

        def main_flow():
          chk("input")
          chk("input_only")
          chk("consts")
          chk("cm1")
          chk("cmall")
          alloc_ffn_ln(reset=False)
          for l in range(L):
            ffn(l, "ffn1_w_in", "ffn1_w_out", "ln1_g", "ln1_b")
            if l == 0:
                tap("x_ln1", x[0])
            chk("ffn1")
            MTP = [m_ for m_ in MT if m_[0] != 4]
            MTS = [m_ for m_ in MT if m_[0] == 4]
            def lane_rw(A_, subt, hps=(0, 1)):
                for hp in hps:
                    yield from rwkv_part(A_, l, hp, subt)

            def lane_hg(B_, hs, subt):
                for h in hs:
                    yield from hgrn_part(B_, l, h, subt)

            def lane_cm(C_, hp, subt):
                yield from cm_part(C_, l, hp, subt)

            arena_reset()
            A1 = rwkv_alloc(False)
            B1 = hgrn_alloc(False)
            print("[arena cols A1]", state["ar"], "of", RCOLS, flush=True)
            run_lanes([lane_rw(A1, MTP), lane_hg(B1, (0, 1, 2, 3), MTP)], [[0, 1, 2, 3, 4], [5, 6, 7]])
            arena_reset()
            A2 = rwkv_alloc(True)
            C0, C1 = cm_alloc(False), cm_alloc(False)
            CS = cm_alloc(True)
            print("[arena cols A2]", state["ar"], "of", RCOLS, flush=True)

            def lane_cm_s(C_):
                for hp_ in range(2):
                    yield from cm_part(C_, l, hp_, MTS)

            run_lanes([lane_rw(A2, MTS, (0,)), lane_cm(C0, 0, MTP), lane_cm(C1, 1, MTP), lane_cm_s(CS)],
                      [[0, 1, 2], [3, 4], [5, 6], [7]])
            arena_reset()
            A3 = rwkv_alloc(True)
            B3 = hgrn_alloc(True)
            print("[arena cols A3]", state["ar"], "of", RCOLS, flush=True)
            run_lanes([lane_rw(A3, MTS, (1,)), lane_hg(B3, (0, 1, 2, 3), MTS)], [[0, 1, 2], [3, 4, 5, 6, 7]])
            chk("cm")
            alloc_ffn_ln()
            layer_norm(l, "ln2_g", "ln2_b")
            if l == 0:
                tap("x_ln2", x[0])
            chk("ln2")
            ffn(l, "ffn2_w_in", "ffn2_w_out", "ln3_g", "ln3_b")
            chk("layer0")

        try:
            main_flow()
        except _Stop:
            pass

        instage = stg
        for bi in range(17 if stop_after not in ("consts", "cm1", "cmall", "input_only") else 0):
            sgi = instage[bi % 2]
            if bi < 16:
                nrow, ti, c0 = 128, bi // 4, (bi % 4) * 128
            else:
                nrow, ti, c0 = 64, 4, 0
            for g in range(2):
                p_ = psum()
                for q in range(4):
                    kc = g * 4 + q
                    k.tr(p_[0:nrow, q * 128:(q + 1) * 128], x[ti][:, kc, c0:c0 + nrow], ident)
                k.copy("act" if g == 0 else "dve", sgi[0:nrow, g * 512:(g + 1) * 512], p_[0:nrow, :])
            if bi < 16:
                k.dma(O["y_p"][bi * 128:(bi + 1) * 128, :], sgi)
            else:
                k.dma(O["y_s"], sgi[0:64, :])
        S.barrier()
        S.replay(block)
        print(f"[build] ops={S.n_ops} waits={S.n_waits}", flush=True)
    return nc


_NC_CACHE = {}


def make_in_maps(inputs):
    f32 = lambda a: np.ascontiguousarray(np.asarray(a, dtype=np.float32))
    consts = host_consts()
    cols = []
    for name, nchunk in PCOL_SPEC:
        a = f32(inputs[name]).reshape(L, nchunk, 128)
        cols.append(a.reshape(L * nchunk, 128))
    pcols = np.ascontiguousarray(np.concatenate(cols, 0).T)
    shared = {n: f32(inputs[n]) for n in ["ffn1_w_in", "ffn1_w_out", "mix_w_in", "mix_w_out", "ffn2_w_in",
                                           "ffn2_w_out", "rw_w_w2", "rw_a_w2", "rw_g_w2", "cm_ws", "cm_bs"]}
    shared["pcols"] = pcols
    for n, v in consts.items():
        shared["c_" + n] = f32(v)
    xp = f32(inputs["x_prompt"])
    xs = f32(inputs["x_sample"])
    srw = f32(inputs["state_rwkv"])
    ssh = f32(inputs["state_rwkv_shift"])
    shg = f32(inputs["state_hgrn"])
    maps = []
    for c in range(NCORES):
        m = dict(shared)
        b0 = c * NSB
        m["xp"] = xp[c]
        m["xs"] = np.ascontiguousarray(xs[b0:b0 + NSB].reshape(NSB * NST, D))
        m["st_rw"] = np.ascontiguousarray(srw[:, b0:b0 + NSB])
        m["st_sh"] = np.ascontiguousarray(ssh[:, b0:b0 + NSB])
        m["st_hg"] = np.ascontiguousarray(shg[:, b0:b0 + NSB])
        maps.append(m)
    return maps


def gather(results):
    cat = lambda name, axis: np.concatenate([np.asarray(r[name], dtype=np.float32) for r in results], axis=axis)
    y_p = np.stack([np.asarray(r["y_p"], np.float32) for r in results], 0)
    y_s = cat("y_s", 0).reshape(NCORES * NSB, NST, D)
    rw_S_p = np.stack([np.asarray(r["rw_S_p"], np.float32) for r in results], 1)
    rw_sh_p = np.stack([np.asarray(r["rw_sh_p"], np.float32) for r in results], 1)
    hg_S_p = np.stack([np.asarray(r["hg_S_p"], np.float32) for r in results], 1)
    rw_S_s = cat("rw_S_s", 1)
    rw_sh_s = cat("rw_sh_s", 1)
    hg_S_s = cat("hg_S_s", 1)
    cm_v_s = cat("cm_v_s", 1)
    return (y_p, y_s, rw_S_p, rw_sh_p, hg_S_p, rw_S_s, rw_sh_s, hg_S_s, cm_v_s)


def kernel(**inputs):
    if "nc" not in _NC_CACHE:
        _NC_CACHE["nc"] = build()
    nc = _NC_CACHE["nc"]
    maps = make_in_maps(inputs)
    res = run_bass_kernel_spmd(nc, maps, core_ids=list(range(NCORES)))
    return gather(res.results)
```

```python
import math
from contextlib import ExitStack
import numpy as np
import concourse.bass as bass
import concourse.mybir as mybir
from concourse.bass_utils import run_bass_kernel_spmd

F32 = mybir.dt.float32
BF16 = mybir.dt.bfloat16
F32R = mybir.dt.float32r
AF = mybir.ActivationFunctionType
ALU = mybir.AluOpType

NCORES = 8
D = 1024
KC = 8
SEQ = 2048
NSB = 16
NST = 4
L = 2
DFF = 2816
NJ = 22
INP = 3584
ALPHA = (2.0 * L) ** 0.25
LN_EPS = 1e-5
GN_EPS = 64e-5
RMS_EPS = 1e-6
WSCALE = -math.exp(-0.5)
TT = [(0, 512), (512, 512), (1024, 512), (1536, 512), (2048, 64)]
NTOK = 2112
MT = [(ti, off, 256) for ti in range(4) for off in (0, 256)] + [(4, 0, 64)]
ROUNDS = [(0, 6), (6, 6), (12, 5), (17, 5)]

COMPUTE = ("pe", "act", "dve", "pool")
ALLENG = ("pe", "act", "dve", "pool", "sp")

PCOL_SPEC = [("ln1_g", 8), ("ln1_b", 8), ("ln2_g", 8), ("ln2_b", 8), ("ln3_g", 8), ("ln3_b", 8),
             ("rw_mu", 8), ("rw_w0", 2), ("rw_a0", 2), ("rw_k_k", 2), ("rw_k_a", 2), ("rw_r_k", 2),
             ("rw_gn_g", 2), ("rw_gn_b", 2), ("hg_lb_logits", 4), ("hg_norm_g", 4),
             ("cm_ln_g", 2), ("cm_ln_b", 2)]
PCOL_OFF = {}
_o = 0
for _n, _c in PCOL_SPEC:
    PCOL_OFF[_n] = (_o, _c)
    _o += L * _c
NPCOL = _o


def pcol_idx(name, l, c):
    o, n = PCOL_OFF[name]
    return o + l * n + c


class Tok:
    __slots__ = ("sem", "val", "know")

    def __init__(self, sem, val, know):
        self.sem, self.val, self.know = sem, val, know


class Buf:
    __slots__ = ("name", "w", "r", "excl")

    def __init__(self, name="", excl=False):
        self.name, self.w, self.r, self.excl = name, None, [], excl


class T:
    __slots__ = ("ap", "buf")

    def __init__(self, ap, buf):
        self.ap, self.buf = ap, buf

    def __getitem__(self, k):
        return T(self.ap[k], self.buf)

    def re(self, s, **kw):
        return T(self.ap.rearrange(s, **kw), self.buf)

    def bc(self, shape):
        return T(self.ap.broadcast_to(shape), self.buf)

    def bitcast(self, dt):
        return T(self.ap.bitcast(dt), self.buf)


class Sched:
    def __init__(self, nc, esems, dsems):
        self.nc = nc
        self.prog = {e: [] for e in ALLENG}
        self.esem = esems
        self.cnt = {e: 0 for e in COMPUTE}
        self.know = {e: {} for e in ALLENG}
        self.dpool = [[s, 0, None] for s in dsems]
        self.drr = 0
        self.n_ops = 0
        self.n_waits = 0
        self.defer = None

    def _learn(self, eng, tok):
        kn = self.know[eng]
        if kn.get(tok.sem.num, 0) < tok.val:
            kn[tok.sem.num] = tok.val
        for k, v in tok.know.items():
            if kn.get(k, 0) < v:
                kn[k] = v

    def _need(self, eng, reads, writes):
        need = {}
        mysem = self.esem[eng].num if eng in self.esem else None

        def req(tok, raw):
            if tok is None:
                return
            if (not raw) and tok.sem.num == mysem:
                return
            k = tok.sem.num
            if k not in need or need[k].val < tok.val:
                need[k] = tok

        for b in reads:
            req(b.w, True)
            if b.excl:
                for r in b.r:
                    req(r, False)
        for b in writes:
            req(b.w, False)
            for r in b.r:
                req(r, False)
        waits = []
        kn = self.know[eng]
        for k, tok in need.items():
            if kn.get(k, 0) >= tok.val:
                continue
            waits.append((tok.sem, tok.val))
            self._learn(eng, tok)
        return waits

    def op(self, eng, fn, reads=(), writes=(), dma=False, cost=0.3, single=True):
        if self.defer is not None:
            self.defer.append((eng, fn, [b for b in reads if b is not None], [b for b in writes if b is not None], dma, cost, single))
            return None
        reads = [b for b in reads if b is not None]
        writes = [b for b in writes if b is not None]
        waits = self._need(eng, reads, writes)
        if dma:
            slot = self.dpool[self.drr % len(self.dpool)]
            self.drr += 1
            sem, val, last = slot
            if last is not None and self.know[eng].get(sem.num, 0) < last.val:
                waits.append((sem, last.val))
                self._learn(eng, last)
            val += 16
            slot[1] = val
            tok = Tok(sem, val, dict(self.know[eng]))
            slot[2] = tok
            inc = 16
        else:
            self.cnt[eng] += 1
            sem = self.esem[eng]
            val = self.cnt[eng]
            self.know[eng][sem.num] = val
            tok = Tok(sem, val, dict(self.know[eng]))
            inc = 1
        self.n_ops += 1
        self.n_waits += len(waits)

        aw = getattr(fn, "accepts_wait", False) and (not dma) and len(waits) > 0
        embed = (not aw) and single and (not dma) and len(waits) > 0

        def emit(E, waits=waits, fn=fn, sem=sem, inc=inc, embed=embed, aw=aw):
            pre = waits[:-1] if (embed or aw) else waits
            for (s, v) in pre:
                E.wait_ge(s, v)
            if aw:
                ins = fn(E, waits[-1])
            else:
                ins = fn(E)
                if embed:
                    ins._wait_ge(waits[-1][0], waits[-1][1])
            ins.then_inc(sem, inc)

        self.prog[eng].append(emit)
        for b in reads:
            b.r.append(tok)
            if len(b.r) > 64:
                b.r = b.r[-64:] if False else b.r
        for b in writes:
            b.w = tok
            b.r = []
        return tok

    def barrier(self):
        toks = []
        for slot in self.dpool:
            if slot[2] is not None:
                toks.append(slot[2])
        for e in COMPUTE:
            if self.cnt[e] > 0:
                toks.append(Tok(self.esem[e], self.cnt[e], {}))
        for eng in ALLENG:
            waits = []
            for tok in toks:
                if self.know[eng].get(tok.sem.num, 0) < tok.val:
                    waits.append((tok.sem, tok.val))
                    self.know[eng][tok.sem.num] = tok.val

            def emit(E, waits=waits):
                for (s, v) in waits:
                    E.wait_ge(s, v)

            if waits:
                self.prog[eng].append(emit)

    def replay(self, block):
        prog = self.prog

        @block.tensor
        def _(E):
            for f in prog["pe"]:
                f(E)

        @block.scalar
        def _(E):
            for f in prog["act"]:
                f(E)

        @block.vector
        def _(E):
            for f in prog["dve"]:
                f(E)

        @block.gpsimd
        def _(E):
            for f in prog["pool"]:
                f(E)

        @block.sync
        def _(E):
            for f in prog["sp"]:
                f(E)


class K:
    def __init__(self, S):
        self.S = S

    @staticmethod
    def _ap(x):
        return x.ap if isinstance(x, T) else x

    @staticmethod
    def _bufs(*xs):
        return [x.buf for x in xs if isinstance(x, T)]

    @staticmethod
    def _cost(eng, out):
        n = 1
        for d in out.ap.shape[1:]:
            n *= int(d)
        if eng == "pool":
            return 0.15 + n / 300.0
        return 0.15 + n / 1000.0

    def tt(self, eng, out, a, b, op):
        self.S.op(eng, lambda E: E.tensor_tensor(out=out.ap, in0=a.ap, in1=b.ap, op=op),
                  reads=self._bufs(a, b), writes=[out.buf], cost=self._cost(eng, out))

    def ts(self, eng, out, a, s1, s2, op0, op1=None):
        s1a, s2a = self._ap(s1), self._ap(s2)
        if op1 is None:
            fn = lambda E: E.tensor_scalar(out=out.ap, in0=a.ap, scalar1=s1a, scalar2=None, op0=op0)
        else:
            fn = lambda E: E.tensor_scalar(out=out.ap, in0=a.ap, scalar1=s1a, scalar2=s2a, op0=op0, op1=op1)
        self.S.op(eng, fn, reads=self._bufs(a, s1, s2), writes=[out.buf], cost=self._cost(eng, out))

    def stt(self, eng, out, a, s, b, op0, op1):
        sa = self._ap(s)
        self.S.op(eng, lambda E: E.scalar_tensor_tensor(out=out.ap, in0=a.ap, scalar=sa, in1=b.ap, op0=op0, op1=op1),
                  reads=self._bufs(a, s, b), writes=[out.buf], cost=self._cost(eng, out))

    def copy(self, eng, out, a):
        if eng == "act":
            fn = lambda E: E.activation(out=out.ap, in_=a.ap, func=AF.Identity)
        else:
            fn = lambda E: E.tensor_copy(out=out.ap, in_=a.ap)
        self.S.op(eng, fn, reads=[a.buf], writes=[out.buf], cost=self._cost(eng, out))

    def act(self, out, a, func, bias=None, scale=1.0):
        ba, sa = self._ap(bias), self._ap(scale)
        if bias is None:
            fn = lambda E: E.activation(out=out.ap, in_=a.ap, func=func, scale=sa)
        else:
            fn = lambda E: E.activation(out=out.ap, in_=a.ap, func=func, bias=ba, scale=sa)
        self.S.op("act", fn, reads=self._bufs(a, bias, scale), writes=[out.buf], cost=self._cost("act", out))

    def memset(self, eng, out, val):
        self.S.op(eng, lambda E: E.memset(out.ap, val), writes=[out.buf], cost=self._cost(eng, out))

    def scan(self, out, d0, d1):
        self.S.op("dve", lambda E: E.tensor_tensor_scan(out=out.ap, data0=d0.ap, data1=d1.ap, initial=0.0,
                                                        op0=ALU.mult, op1=ALU.add),
                  reads=[d0.buf, d1.buf], writes=[out.buf], cost=self._cost("dve", out))

    def mm(self, out, pairs, extra_reads=(), tp=None):
        n = len(pairs)

        def fn(E, w=None):
            ins = None
            for i, (l, r) in enumerate(pairs):
                if tp is None:
                    ins = E.matmul(out.ap, lhsT=l.ap, rhs=r.ap, start=(i == 0), stop=(i == n - 1))
                else:
                    ins = E.matmul(out.ap, lhsT=l.ap, rhs=r.ap, start=(i == 0), stop=(i == n - 1), tile_position=tp)
                if i == 0 and w is not None:
                    ins._wait_ge(w[0], w[1])
            return ins

        fn.accepts_wait = True

        rd = []
        for (l, r) in pairs:
            rd += [l.buf, r.buf]
        ncol = int(out.ap.shape[-1])
        self.S.op("pe", fn, reads=rd + list(extra_reads), writes=[out.buf], cost=n * (0.06 + ncol / 2400.0), single=(n == 1))

    def tr(self, out, a, ident):
        self.S.op("pe", lambda E: E.transpose(out.ap, a.ap, ident.ap), reads=[a.buf, ident.buf], writes=[out.buf], cost=0.25)

    def dma(self, out, a, eng="sp", slow=False):
        oa, ia = self._ap(out), self._ap(a)
        if slow:
            fn = lambda E: E.dma_start(out=oa, in_=ia, allow_slow_non_contiguous=True)
        else:
            fn = lambda E: E.dma_start(out=oa, in_=ia)
        self.S.op(eng, fn, reads=self._bufs(a), writes=self._bufs(out), dma=True, cost=2.5)


def host_consts():
    c = {}
    c["ident"] = np.eye(128, dtype=np.float32)
    s = np.arange(64)[:, None]
    t = np.arange(64)[None, :]
    strict = (s < t).astype(np.float32)
    incl = (s <= t).astype(np.float32)
    same = ((s // 4) == (t // 4)).astype(np.float32)
    gm = np.stack([np.concatenate([strict, incl, strict, incl], 1),
                   np.concatenate([strict * same, incl * same, strict * same, incl * same], 1)], 0)
    c["gmask"] = gm.astype(np.float32)
    c["ntmask"] = np.stack([strict.T, (strict * same).T], 0).astype(np.float32)
    s2 = np.arange(128)[:, None]
    t2 = np.arange(128)[None, :]
    c["cmask"] = (s2 <= t2).astype(np.float32)
    c["cmask_s"] = (incl * same).astype(np.float32)
    rm = np.ones((128, 512), np.float32)
    rm[:, 0::64] = 0.0
    c["rmask"] = rm
    rs = np.ones((128, 64), np.float32)
    rs[:, 0::4] = 0.0
    c["rmask_s"] = rs
    cm = np.zeros((128, 16, 64), np.float32)
    for b in range(16):
        cm[:, b, 4 * b:4 * b + 4] = 1.0
    c["colmask"] = cm.reshape(128, 1024)
    rw = np.zeros((64, 16), np.float32)
    for b in range(16):
        rw[4 * b:4 * b + 4, b] = 1.0
    c["rowmask"] = rw
    blk = np.zeros((128, 128), np.float32)
    blk[:64, :64] = 1.0
    blk[64:, 64:] = 1.0
    c["blk1"] = blk
    c["e4"] = np.tile(np.eye(4, dtype=np.float32), (1, 16))
    return c


CONST_SHAPES = {"ident": [128, 128], "gmask": [2, 64, 256], "ntmask": [2, 64, 64], "cmask": [128, 128],
                "cmask_s": [64, 64], "rmask": [128, 512], "rmask_s": [128, 64], "colmask": [128, 1024],
                "rowmask": [64, 16], "blk1": [128, 128], "e4": [4, 64]}

IN_SHAPES = {
    "xp": [SEQ, D], "xs": [NSB * NST, D], "st_rw": [L, NSB, 4, 64, 64], "st_sh": [L, NSB, D],
    "st_hg": [L, NSB, 4, 128, 128],
    "ffn1_w_in": [L, D, 2 * DFF], "ffn1_w_out": [L, DFF, D], "mix_w_in": [L, D, INP], "mix_w_out": [L, D, D],
    "ffn2_w_in": [L, D, 2 * DFF], "ffn2_w_out": [L, DFF, D],
    "rw_w_w2": [L, 64, 256], "rw_a_w2": [L, 64, 256], "rw_g_w2": [L, 128, 256],
    "cm_ws": [L, 4, 128, 128], "cm_bs": [L, 4, 128], "pcols": [128, NPCOL],
}
OUT_SHAPES = {
    "y_p": [SEQ, D], "y_s": [NSB * NST, D], "rw_S_p": [L, 4, 64, 64], "rw_sh_p": [L, D],
    "hg_S_p": [L, 4, 128, 128], "rw_S_s": [L, NSB, 4, 64, 64], "rw_sh_s": [L, NSB, D],
    "hg_S_s": [L, NSB, 4, 128, 128], "cm_v_s": [L, NSB, NST, 256],
}


def build(taps=None, stop_after=None):
    taps = taps or {}
    nc = bass.Bass("TRN2", target_bir_lowering=False)
    I = {n: nc.dram_tensor(n, s, F32, kind="ExternalInput").ap() for n, s in IN_SHAPES.items()}
    Cn = {n: nc.dram_tensor("c_" + n, s, F32, kind="ExternalInput").ap() for n, s in CONST_SHAPES.items()}
    O = {n: nc.dram_tensor(n, s, F32, kind="ExternalOutput").ap() for n, s in OUT_SHAPES.items()}
    TAPO = {n: nc.dram_tensor("tap_" + n, s, F32, kind="ExternalOutput").ap() for n, s in taps.items()}

    with ExitStack() as st:
        def sb(name, shape, dt=F32, parts=128):
            h = st.enter_context(nc.sbuf_tensor("sb_" + name, [parts] + list(shape), dt))
            return T(h[:], Buf(name))

        x = [sb(f"x{t}", [KC, n]) for t, (_, n) in enumerate(TT)]
        xb = [sb(f"xb{t}", [KC, n], BF16) for t, (_, n) in enumerate(TT)]
        RCOLS = 17664
        Rh = st.enter_context(nc.sbuf_tensor("arena", [128, RCOLS], F32))
        stg = [sb(f"stg{i}", [1024]) for i in range(2)]
        wbf = [sb(f"wbf{i}", [1024], BF16) for i in range(6)]
        pcols = sb("pcols", [NPCOL])
        ident = sb("ident", [128])
        gmask = sb("gmask", [2, 256])
        ntmask = sb("ntmask", [2, 64])
        cmask = sb("cmask", [128])
        cmask_s = sb("cmask_s", [64], parts=64)
        rmask = sb("rmask", [256])
        rmask_s = sb("rmask_s", [64])
        colmask = sb("colmask", [16, 64], BF16)
        rowmask = sb("rowmask", [16], parts=64)
        blk1f = sb("blk1f", [128])
        blk1 = sb("blk1", [128], BF16)
        blk64 = sb("blk64", [128], BF16)
        onesD = sb("onesD", [128], BF16)
        onesM = sb("onesM", [128], BF16)
        ones_row = sb("ones_row", [128], BF16, parts=1)
        e4 = sb("e4", [64], parts=4)
        epsc = sb("epsc", [8])
        pcA = sb("pcA", [2 * L * 8])
        lbc = sb("lbc", [L * 4])
        omlc = sb("omlc", [L * 4])
        wa2 = sb("wa2", [L, 256], BF16)
        gw2 = sb("gw2", [L, 256], BF16)
        wct = sb("wct", [L * 4, 128], BF16)
        wcs = sb("wcs", [L * 4, 64], BF16, parts=64)
        bsrow = sb("bsrow", [L * 4 * 128], BF16, parts=1)
        bss = sb("bss", [L * 4, 64], BF16, parts=1)
        psh = [st.enter_context(nc.psum_tensor(f"ps{i}", [128, 512], F32)) for i in range(8)]
        ps = [T(h[:], Buf(f"ps{i}", excl=True)) for i, h in enumerate(psh)]
        esems = {e: st.enter_context(nc.semaphore(f"s_{e}")) for e in COMPUTE}
        dsems = [st.enter_context(nc.semaphore(f"d{i}")) for i in range(24)]
        block = st.enter_context(nc.Block())
        S = Sched(nc, esems, dsems)
        k = K(S)

        import os

        class _Stop(Exception):
            pass

        state = {"ps": 0, "stg": 0, "wbf": 0, "ar": 0, "eng": 0}

        def psum():
            sub = state.get("banks")
            if sub is None:
                state["ps"] += 1
                return ps[state["ps"] % 8]
            ln = state["lane"]
            lst = sub[ln]
            key = ("psl", ln)
            state[key] = state.get(key, -1) + 1
            return ps[lst[state[key] % len(lst)]]

        def arena_reset():
            S.barrier()
            state["ar"] = 0

        def ar(shape, dt=F32, parts=128, name="ar"):
            nfree = int(np.prod(shape))
            nbytes = nfree * (4 if dt == F32 else 2)
            cols = (nbytes + 63) // 64 * 16
            a = state["ar"]
            assert a + cols <= RCOLS, ("arena overflow", a, cols)
            state["ar"] = a + cols
            ap = Rh[0:parts, a:a + cols]
            if dt != F32:
                ap = ap.bitcast(dt)
            ap = ap[:, 0:nfree]
            if len(shape) == 2:
                ap = ap.rearrange("p (a b) -> p a b", a=shape[0])
            elif len(shape) == 3:
                ap = ap.rearrange("p (a b c) -> p a b c", a=shape[0], b=shape[1])
            return T(ap, Buf(name))

        def pc(name, l, c, parts=slice(0, 128)):
            i = pcol_idx(name, l, c)
            return pcols[parts, i:i + 1]

        def tap(name, t):
            if name in TAPO:
                k.dma(TAPO[name], t)

        def load_w(src, a, b):
            sg_ = stg[state["stg"] % 2]
            state["stg"] += 1
            wb_ = wbf[state["wbf"] % 6]
            state["wbf"] += 1
            sv = sg_[:, 0:a * b].re("p (a b) -> p a b", a=a)
            wv = wb_[:, 0:a * b].re("p (a b) -> p a b", a=a)
            k.dma(sv, src)
            k.copy("pool", wv, sv)
            return wv

        def load_w_to(dst, src, a, b, sg_=None):
            if sg_ is None:
                sg_ = stg[state.get("lane", 0) % 2]
            sv = sg_[:, 0:a * b].re("p (a b) -> p a b", a=a)
            wv = dst[:, 0:a * b].re("p (a b) -> p a b", a=a)
            k.dma(sv, src)
            k.copy("act", wv, sv)
            return wv

        def win_cols(l, c0):
            return I["mix_w_in"][l][:, c0:c0 + 128].rearrange("(kc p) m -> p kc m", p=128)

        k.dma(pcols, I["pcols"])
        k.dma(ident, Cn["ident"])
        for ph in range(2):
            k.dma(gmask[ph * 64:(ph + 1) * 64], Cn["gmask"].rearrange("a p n -> p a n"))
            k.dma(ntmask[ph * 64:(ph + 1) * 64], Cn["ntmask"].rearrange("a p n -> p a n"))
        k.dma(cmask, Cn["cmask"])
        k.dma(cmask_s, Cn["cmask_s"])
        k.dma(rmask, Cn["rmask"][:, 0:256])
        k.dma(rmask_s, Cn["rmask_s"])
        k.dma(rowmask, Cn["rowmask"])
        k.dma(blk1f, Cn["blk1"])
        k.dma(e4, Cn["e4"])
        k.copy("dve", blk1, blk1f)
        k.ts("dve", blk64, blk1f, 1.0 / 64.0, None, ALU.mult)
        k.memset("pool", onesD, 1.0 / 128.0)
        k.memset("pool", onesM, 1.0 / 1024.0)
        k.memset("pool", ones_row, 1.0)
        for i, v in enumerate([LN_EPS, GN_EPS, RMS_EPS, 1e-24, 1.0, 0.0]):
            k.memset("pool", epsc[:, i:i + 1], v)
        EPS_LN, EPS_GN, EPS_RMS, EPS_TINY = (epsc[:, i:i + 1] for i in range(4))
        k.ts("dve", pcA, pcols[:, 0:2 * L * 8], ALPHA, None, ALU.mult)
        k.memset("pool", lbc, 0.0)
        o_lb, _n = PCOL_OFF["hg_lb_logits"]
        k.tt("dve", lbc[:, 4:8], pcols[:, o_lb + 4:o_lb + 8], pcols[:, o_lb:o_lb + 4], ALU.subtract)
        k.act(lbc[:, 4:8], lbc[:, 4:8], AF.Sigmoid)
        k.ts("dve", omlc, lbc, -1.0, 1.0, ALU.mult, ALU.add)
        wa2f = ar([L, 256], name="wa2f")
        gw2f = ar([L, 256], name="gw2f")
        bsrow_f = stg[1][0:1, :]
        bss_f = ar([L * 4, 64], parts=1, name="bss_f")
        instage = stg
        colmask_f = stg[0].re("p (a b) -> p a b", a=16)
        k.dma(colmask_f, Cn["colmask"].rearrange("p (a b) -> p a b", a=16))
        k.copy("pool", colmask, colmask_f)
        for l in range(L):
            k.dma(wa2f[0:64, l, :], I["rw_w_w2"][l])
            k.dma(wa2f[64:128, l, :], I["rw_a_w2"][l])
            k.dma(gw2f[:, l, :], I["rw_g_w2"][l])
        k.copy("pool", wa2, wa2f)
        k.copy("pool", gw2, gw2f)
        k.dma(bsrow_f, I["cm_bs"].rearrange("l h t -> (l h t)").unsqueeze(0))
        k.copy("dve", bsrow, bsrow_f)
        for l in range(L if stop_after != "consts" else 0):
            for h in range(4 if stop_after != "cm1" else 1):
                li = l * 4 + h
                sg_ = stg[state["stg"] % 2]
                state["stg"] += 1
                k.dma(sg_[:, 0:128], I["cm_ws"][l, h])
                p_ = psum()
                k.tr(p_[:, 0:128], sg_[:, 0:128], ident)
                k.tt("dve", wct[:, li, :], p_[:, 0:128], cmask, ALU.mult)
                k.dma(sg_[0:4, 128:132], I["cm_ws"][l, h, 0:4, 0:4].rearrange("t s -> s t"), slow=True)
                p2 = psum()
                k.mm(p2[0:64, 0:4], [(e4[0:4, :], sg_[0:4, 128:132])])
                k.copy("act", sg_[0:64, 136:140], p2[0:64, 0:4])
                k.tt("dve", wcs[:, li, :].re("p (a b) -> p a b", a=16),
                     sg_[0:64, 136:140].re("p (o b) -> p o b", o=1).bc([64, 16, 4]),
                     cmask_s.re("p (a b) -> p a b", a=16), ALU.mult)
                k.dma(bss_f[0:1, li, :].re("p (a b) -> p a b", a=16),
                      I["cm_bs"][l, h:h + 1, 0:4].unsqueeze(1).broadcast_to([1, 16, 4]))
        k.copy("dve", bss, bss_f)

        import os
        _nb = int(os.environ.get("DBG_NB", "17"))
        _noact = os.environ.get("DBG_NOACT", "0") == "1"
        for bi in range(_nb if stop_after not in ("consts", "cm1", "cmall") else 0):
            sgi = instage[bi % 2]
            if bi < 16:
                nrow, ti, c0 = 128, bi // 4, (bi % 4) * 128
                k.dma(sgi, I["xp"][bi * 128:(bi + 1) * 128, :])
            else:
                nrow, ti, c0 = 64, 4, 0
                k.dma(sgi[0:64, :], I["xs"])
            for g in range(2):
                p_ = psum()
                pv = p_.re("p (a b) -> p a b", a=4)
                for q in range(4):
                    kc = g * 4 + q
                    k.tr(pv[:, q, 0:nrow], sgi[0:nrow, kc * 128:(kc + 1) * 128], ident[0:nrow, 0:nrow])
                _m = os.environ.get("DBG_MODE", "b")
                if _m == "a":
                    for q in range(4):
                        k.copy("act", x[ti][:, g * 4 + q, c0:c0 + nrow], pv[:, q, 0:nrow])
                    k.copy("pool", xb[ti][:, g * 4:(g + 1) * 4, c0:c0 + nrow], x[ti][:, g * 4:(g + 1) * 4, c0:c0 + nrow])
                elif _m == "b":
                    k.copy("dve", x[ti][:, g * 4:(g + 1) * 4, c0:c0 + nrow], pv[:, :, 0:nrow])
                    k.copy("act", xb[ti][:, g * 4:(g + 1) * 4, c0:c0 + nrow], x[ti][:, g * 4:(g + 1) * 4, c0:c0 + nrow])

        SCR = {}

        def alloc_ffn_ln(reset=True):
            if reset:
                arena_reset()
            SCR["usq"] = [ar([KC, 512], BF16, name=f"usq{i}") for i in range(2)]
            SCR["lnt"] = [[ar([512], name=f"lnt{i}_{j}") for j in range(4)] for i in range(2)]
            SCR["hbuf"] = [[ar([n], BF16, name=f"h{jj}_{ti}") for ti, (t0, n) in enumerate(TT)] for jj in range(6)]
            SCR["silu"] = [ar([512], name=f"silu_tmp{i}") for i in range(2)]

        def ln_lane(l, gname, bname, tiles, li, xscale=False):
            usq_all, lnt = SCR["usq"][li], SCR["lnt"][li]
            for ti in tiles:
                n = TT[ti][1]
                usq = usq_all[:, :, 0:n]
                k.act(xb[ti], x[ti], AF.Identity)
                k.act(usq, x[ti], AF.Square)
                yield
                pm = psum()
                k.mm(pm[:, 0:n], [(onesM, xb[ti][:, kc, :]) for kc in range(KC)])
                pq = psum()
                k.mm(pq[:, 0:n], [(onesM, usq[:, kc, :]) for kc in range(KC)])
                mean, var, rstd, nmr = (t_[:, 0:n] for t_ in lnt)
                k.copy("act", mean, pm[:, 0:n])
                k.act(var, pm[:, 0:n], AF.Square)
                k.tt("dve", var, pq[:, 0:n], var, ALU.subtract)
                yield
                k.act(rstd, var, AF.Ln, bias=EPS_LN)
                k.act(rstd, rstd, AF.Exp, scale=-0.5)
                k.stt("dve", nmr, mean, -1.0, rstd, ALU.mult, ALU.mult)
                k.tt("dve", x[ti], x[ti], rstd.re("p (o n) -> p o n", o=1).bc([128, KC, n]), ALU.mult)
                k.tt("dve", x[ti], x[ti], nmr.re("p (o n) -> p o n", o=1).bc([128, KC, n]), ALU.add)
                yield
                for kc in range(KC):
                    k.act(xb[ti][:, kc, :], x[ti][:, kc, :], AF.Identity, bias=pc(bname, l, kc), scale=pc(gname, l, kc))
                for kc in range(KC):
                    if xscale:
                        gcol = pcA[:, l * 8 + kc:l * 8 + kc + 1]
                        bcol = pcA[:, L * 8 + l * 8 + kc:L * 8 + l * 8 + kc + 1]
                    else:
                        gcol, bcol = pc(gname, l, kc), pc(bname, l, kc)
                    k.ts("dve", x[ti][:, kc, :], x[ti][:, kc, :], gcol, bcol, ALU.mult, ALU.add)
                yield

        def layer_norm(l, gname, bname):
            xs_ = (gname == "ln1_g")
            run_lanes([ln_lane(l, gname, bname, (0, 2, 4), 0, xs_), ln_lane(l, gname, bname, (1, 3), 1, xs_)],
                      [[0, 1, 2, 3], [4, 5, 6, 7]])

        def ffn(l, wi, wo, gname, bname):
            hbuf, silu_tmp = SCR["hbuf"], SCR["silu"]
            _f = os.environ.get("DBG_FFN", "")
            for (j0, nj) in ROUNDS:
                for jj in range(nj):
                    j = j0 + jj
                    if _f in ("gu1",) and jj > 0:
                        raise _Stop()
                    wg = load_w(I[wi][l][:, j * 128:(j + 1) * 128].rearrange("(kc p) m -> p kc m", p=128), KC, 128)
                    wu = load_w(I[wi][l][:, DFF + j * 128:DFF + (j + 1) * 128].rearrange("(kc p) m -> p kc m", p=128), KC, 128)
                    for ti, (t0, n) in enumerate(TT):
                        pg = psum()
                        k.mm(pg[:, 0:n], [(wg[:, kc, :], xb[ti][:, kc, :]) for kc in range(KC)])
                        pu = psum()
                        k.mm(pu[:, 0:n], [(wu[:, kc, :], xb[ti][:, kc, :]) for kc in range(KC)])
                        state["eng"] += 1
                        stmp = silu_tmp[state["eng"] % 2][:, 0:n]
                        if _f == "gu1_nosilu":
                            raise _Stop()
                        k.act(stmp, pg[:, 0:n], AF.Silu)
                        if _f == "gu1_nomul":
                            raise _Stop()
                        k.stt("dve", hbuf[jj][ti], stmp, 0.5, pu[:, 0:n], ALU.mult, ALU.mult)
                if _f == "gu":
                    raise _Stop()
                for m in range(KC):
                    wo_ = load_w(I[wo][l][j0 * 128:(j0 + nj) * 128, m * 128:(m + 1) * 128].rearrange("(j p) m -> p j m", p=128), nj, 128)
                    for ti, (t0, n) in enumerate(TT):
                        po = psum()
                        k.mm(po[:, 0:n], [(wo_[:, jj, :], hbuf[jj][ti]) for jj in range(nj)])
                        if j0 == 0:
                            k.stt("dve", x[ti][:, m, :], x[ti][:, m, :], ALPHA, po[:, 0:n], ALU.mult, ALU.add)
                        else:
                            k.tt("dve", x[ti][:, m, :], x[ti][:, m, :], po[:, 0:n], ALU.add)
            if _f == "noln":
                raise _Stop()
            layer_norm(l, gname, bname)

        def contrib(outp, wo_, ti, off, n, first=False):
            for m in range(KC):
                po = psum()
                k.mm(po[:, 0:n], [(wo_[:, m, :], outp)])
                if first:
                    k.stt("dve", x[ti][:, m, off:off + n], x[ti][:, m, off:off + n], ALPHA, po[:, 0:n], ALU.mult, ALU.add)
                else:
                    k.tt("dve", x[ti][:, m, off:off + n], x[ti][:, m, off:off + n], po[:, 0:n], ALU.add)

        def project(wlist, ti, off, n, outs, funcs=None):
            for i, w_ in enumerate(wlist):
                p_ = psum()
                k.mm(p_[:, 0:n], [(w_[:, kc, :], xb[ti][:, kc, off:off + n]) for kc in range(KC)])
                f = None if funcs is None else funcs[i]
                if f is None:
                    k.copy("act" if i % 2 == 0 else "dve", outs[i][:, 0:n], p_[:, 0:n])
                else:
                    k.act(outs[i][:, 0:n], p_[:, 0:n], f)

        def head_stats(src, n, eps, blk, tmp_b, tmp_sq, mean, rstd, tmpf, one_bank=False):
            k.act(tmp_b[:, 0:n], src, AF.Identity)
            k.act(tmp_sq[:, 0:n], src, AF.Square)
            if one_bank:
                pb = psum()
                pm, pq = pb[:, 0:256], pb[:, 256:512]
            else:
                pm, pq = psum(), psum()
            k.mm(pm[:, 0:n], [(blk, tmp_b[:, 0:n])])
            k.mm(pq[:, 0:n], [(blk, tmp_sq[:, 0:n])])
            k.copy("act", mean, pm[:, 0:n])
            k.act(tmpf, mean, AF.Square)
            k.tt("dve", tmpf, pq[:, 0:n], tmpf, ALU.subtract)
            k.act(rstd, tmpf, AF.Ln, bias=eps)
            k.act(rstd, rstd, AF.Exp, scale=-0.5)

        def rwkv_alloc(sample):
            A = {}
            nb_ = 64 if sample else 256
            ncq = 1 if sample else 4
            A["W"] = wbf
            for nm in ["zr", "zk", "zv", "zwa", "zg", "dd", "sr", "sk", "sv", "swa", "sgg",
                       "logw", "aa", "oo", "t1", "t2"]:
                A[nm] = ar([nb_], name=nm)
            for nm in ["gg", "bsum"]:
                A[nm] = ar([nb_], BF16, name=nm)
            A["mean"], A["rstd"], A["En"], A["Epm"], A["Et"] = A["logw"], A["aa"], A["swa"], A["sgg"], A["dd"]
            for nm in ["twx", "sgb", "sqb", "obb", "osq", "outp"]:
                A[nm] = ar([nb_], BF16, name=nm)
            A["AR"] = ar([ncq, 128], BF16, name="AR")
            A["BK"] = ar([ncq, 2, 64], BF16, name="BK")
            A["BKd"] = ar([ncq, 2, 64], name="BKd")
            A["Vt"] = [ar([128], BF16, parts=64, name=f"Vt{c}") for c in range(ncq)]
            A["Bdt"] = [ar([128], BF16, parts=64, name=f"Bdt{c}") for c in range(ncq)]
            A["Kdt"] = [ar([128], BF16, parts=64, name=f"Kdt{c}") for c in range(ncq)]
            A["zlast"] = ar([8], name="zlast")
            A["S32"] = ar([128], name="S32")
            A["Sbf"] = ar([128], BF16, name="Sbf")
            A["sst"] = ar([4, 128], name="sst")
            if sample:
                A["S32s"] = ar([16, 128], name="S32s")
                A["Sbfs"] = ar([16, 128], BF16, name="Sbfs")
                A["ARm"] = ar([2, 16, 64], BF16, name="ARm")
                A["sh0"] = ar([5, 16], name="sh0")
                A["Uwm"] = ar([16, 64], BF16, parts=64, name="Uwm")
                A["Vtm"] = ar([16, 64], BF16, parts=64, name="Vtm")
            A["chain"] = []
            for i in range(ncq):
                u = {}
                for nm in ["P0", "P1", "PT0", "PT1", "T0", "T1"]:
                    u[nm] = ar([128], BF16, name=nm + str(i))
                k.memset("pool", u["P0"], 0.0)
                k.memset("pool", u["PT0"], 0.0)
                A["chain"].append(u)
            A["ures"] = {}
            A["Tbd"] = [ar([128], BF16, name=f"Tbd{c}") for c in range(ncq)]
            for c in range(ncq):
                for h2 in range(2):
                    A["ures"][(c, h2)] = {"GAb": ar([192], BF16, parts=64, name=f"GAb{c}{h2}")}
            A["Xbf"] = ar([64], BF16, name="Xbf")
            A["Uw"] = ar([128], BF16, parts=64, name="Uw")
            return A

        def rwkv_stageA(A, chunks, sample):
            mi = 1 if sample else 0
            AR, BK = A["AR"], A["BK"]
            nl = 1 if sample else 5
            h0, h1 = slice(0, 64), slice(64, 128)
            st_ = [{"c": c, "t": A["chain"][c]} for c in chunks]
            for g0 in range(0, len(st_), 2):
                grp = st_[g0:g0 + 2]
                yield
                for u in grp:
                    c = u["c"]
                    u["pG0"], u["pG1"] = psum(), psum()
                    k.mm(u["pG0"][0:64, 0:128], [(BK[h0, c, 0, :], AR[h0, c, :])])
                    k.mm(u["pG1"][0:64, 0:128], [(BK[h1, c, 0, :], AR[h1, c, :])])
                    k.mm(u["pG0"][0:64, 128:256], [(BK[h0, c, 1, :], AR[h0, c, :])])
                    k.mm(u["pG1"][0:64, 128:256], [(BK[h1, c, 1, :], AR[h1, c, :])])
                yield
                for u in grp:
                    c = u["c"]
                    k.tt("dve", u["t"]["P0"][h0, 0:64], u["pG0"][0:64, 0:64], gmask[h0, mi, 0:64], ALU.mult)
                    k.tt("dve", A["ures"][(c, 0)]["GAb"], u["pG0"][0:64, 64:256], gmask[h0, mi, 64:256], ALU.mult)
                    k.tt("dve", A["ures"][(c, 1)]["GAb"], u["pG1"][0:64, 64:256], gmask[h0, mi, 64:256], ALU.mult)
            for g0 in range(0, len(st_), 2):
                grp = st_[g0:g0 + 2]
                yield
                for u in grp:
                    c = u["c"]
                    u["pN0"], u["pN1"] = psum(), psum()
                    k.mm(u["pN0"][0:64, 0:64], [(AR[h0, c, 0:64], BK[h0, c, 0, :])])
                    k.mm(u["pN1"][64:128, 0:64], [(AR[h1, c, 0:64], BK[h1, c, 0, :])], tp=(64, 64))
                    k.mm(u["pN1"][64:128, 64:128], [(BK[h1, c, 0, :], AR[h1, c, 0:64])], tp=(64, 64))
                yield
                for u in grp:
                    t = u["t"]
                    k.tt("dve", t["PT0"][h0, 0:64], u["pN0"][0:64, 0:64], ntmask[h0, mi, :], ALU.mult)
                    k.tt("dve", t["PT0"][h1, 64:128], u["pN1"][64:128, 0:64], ntmask[h1, mi, :], ALU.mult)
                    k.tt("dve", t["P0"][h1, 64:128], u["pN1"][64:128, 64:128], gmask[h1, mi, 0:64], ALU.mult)
                    k.tt("dve", t["T0"], t["P0"], ident, ALU.add)
                    u["P"], u["PT"], u["Tm"] = t["P0"], t["PT0"], t["T0"]
            for lv in range(1, nl + 1):
                yield
                for u in st_:
                    t = u["t"]
                    u["Pn"] = t["P1"] if u["P"] is t["P0"] else t["P0"]
                    u["PTn"] = t["PT1"] if u["PT"] is t["PT0"] else t["PT0"]
                    u["Tn"] = t["T1"] if u["Tm"] is t["T0"] else t["T0"]
                    u["p12"] = psum()
                    k.mm(u["p12"][:, 0:128], [(u["P"], u["PT"])])
                    if lv < nl:
                        k.mm(u["p12"][:, 128:256], [(u["PT"], u["P"])])
                yield
                for j, u in enumerate(st_):
                    ev = "act" if j % 2 == 0 else "dve"
                    k.copy(ev, u["PTn"], u["p12"][:, 0:128])
                    if lv < nl:
                        k.copy(ev, u["Pn"], u["p12"][:, 128:256])
                yield
                for u in st_:
                    u["p3"] = psum()
                    k.mm(u["p3"][:, 0:128], [(u["PTn"], u["Tm"])])
                yield
                for u in st_:
                    if lv < nl:
                        k.tt("dve", u["Tn"], u["Tm"], u["p3"][:, 0:128], ALU.add)
                    else:
                        k.tt("dve", A["Tbd"][u["c"]], u["Tm"], u["p3"][:, 0:128], ALU.add)
                    u["P"], u["PT"], u["Tm"] = u["Pn"], u["PTn"], u["Tn"]

        def rwkv_stageB(A, c, sample):
            AR = A["AR"]
            Vt = A["Vt"][c]
            H = []
            for h2 in range(2):
                hr = slice(h2 * 64, h2 * 64 + 64)
                res = A["ures"][(c, h2)]
                GAb = res["GAb"]
                H.append({"hr": hr, "h2": h2, "res": res, "A_rb": GAb[:, 0:64], "A_ak": GAb[:, 64:128], "A_rk": GAb[:, 128:192],
                          "Xbf": A["Xbf"][hr, :], "T": A["Tbd"][c][hr, hr]})
            yield
            for u in H:
                hc = u["hr"]
                u["pX"] = psum()
                if sample:
                    pairs = [(A["ARm"][:, 0, b, :], A["Sbfs"][:, b, hc]) for b in range(16)]
                else:
                    pairs = [(AR[:, c, 0:64], A["Sbf"][:, hc])]
                pairs.append((u["A_ak"], Vt[:, hc]))
                if u["h2"] == 0:
                    k.mm(u["pX"][0:64, 0:64], pairs)
                else:
                    k.mm(u["pX"][64:128, 0:64], pairs, tp=(0, 64))
            yield
            for u in H:
                k.copy("act", u["Xbf"], u["pX"][u["hr"], 0:64])
            yield
            for u in H:
                u["pU"] = psum()
                k.mm(u["pU"][0:64, 0:64], [(u["T"], u["Xbf"])])
            yield
            for u in H:
                k.copy("act", A["Uw"][:, u["hr"]], u["pU"][0:64, 0:64])
            yield
            for u in H:
                u["pO"] = psum()
                if sample:
                    pairs = [(A["Sbfs"][:, b, :], A["ARm"][:, 1, b, :]) for b in range(16)]
                else:
                    pairs = [(A["Sbf"], AR[:, c, 64:128])]
                pairs += [(A["Uw"], u["A_rb"]), (Vt, u["A_rk"])]
                k.mm(u["pO"][:, 0:64], pairs)
            if not sample:
                yield
                for u in H:
                    u["pS"] = psum()
                    k.mm(u["pS"][:, 0:64], [(A["Bdt"][c], A["Uw"][:, u["hr"]]), (A["Kdt"][c], Vt[:, u["hr"]])])
                yield
                for u in H:
                    hr = u["hr"]
                    k.stt("dve", A["S32"][hr, hr], A["S32"][hr, hr], A["Ep"][hr, c * 64 + 63:c * 64 + 64], u["pS"][hr, 0:64],
                          ALU.mult, ALU.add)
                yield
                for u in H:
                    hr = u["hr"]
                    k.copy("act", A["oo"][hr, c * 64:(c + 1) * 64], u["pO"][hr, 0:64])
                yield
                for u in H:
                    hr = u["hr"]
                    k.copy("act", A["Sbf"][hr, hr], A["S32"][hr, hr])
            else:
                yield
                for u in H:
                    hr = u["hr"]
                    k.copy("act", A["oo"][hr, c * 64:(c + 1) * 64], u["pO"][hr, 0:64])
                yield
                for u in H:
                    hr = u["hr"]
                    hc = hr
                    k.tt("dve", A["Uwm"], A["Uw"][:, hc].re("p (o v) -> p o v", o=1).bc([64, 16, 64]),
                         rowmask.re("p (b o) -> p b o", o=1).bc([64, 16, 64]), ALU.mult)
                    k.tt("dve", A["Vtm"], Vt[:, hc].re("p (o v) -> p o v", o=1).bc([64, 16, 64]),
                         rowmask.re("p (b o) -> p b o", o=1).bc([64, 16, 64]), ALU.mult)
                    Eps = A["Ep"][:, 0:64].re("p (b t) -> p b t", t=4)
                    for g in range(2):
                        pS = psum()
                        for b8 in range(8):
                            b = g * 8 + b8
                            k.mm(pS[:, b8 * 64:(b8 + 1) * 64], [(A["Bdt"][0], A["Uwm"][:, b, :]), (A["Kdt"][0], A["Vtm"][:, b, :])])
                        bs_ = slice(g * 8, g * 8 + 8)
                        k.tt("dve", A["S32s"][hr, bs_, hc], A["S32s"][hr, bs_, hc], Eps[hr, bs_, 3:4].bc([64, 8, 64]), ALU.mult)
                        k.tt("dve", A["S32s"][hr, bs_, hc], A["S32s"][hr, bs_, hc],
                             pS[hr, :].re("p (b v) -> p b v", b=8), ALU.add)

        def rwkv_part(A, l, hp, subtiles):
            Wr = load_w_to(A["W"][0], win_cols(l, hp * 128), KC, 128)
            Wk = load_w_to(A["W"][1], win_cols(l, 256 + hp * 128), KC, 128)
            Wv = load_w_to(A["W"][2], win_cols(l, 512 + hp * 128), KC, 128)
            Wwa = load_w_to(A["W"][3], win_cols(l, 768), KC, 128)
            Wg = load_w_to(A["W"][4], win_cols(l, 896), KC, 128)
            Wo = load_w_to(A["W"][5], I["mix_w_out"][l][hp * 128:(hp + 1) * 128, :].rearrange("p (a b) -> p a b", a=KC), KC, 128)
            yield
            zchunk = [hp, 2 + hp, 4 + hp, 6, 7]
            zs = [A["zr"], A["zk"], A["zv"], A["zwa"], A["zg"]]
            ss = [A["sr"], A["sk"], A["sv"], A["swa"], A["sgg"]]
            k.memset("pool", A["S32"], 0.0)
            k.memset("pool", A["Sbf"], 0.0)
            k.memset("pool", A["Uw"], 0.0)
            k.memset("pool", A["zlast"], 0.0)
            k.memset("pool", A["sst"], 0.0)
            for (ti, off, n) in subtiles:
                sample = ti == 4
                lastp = (ti == 3 and off == 256)
                nch = n // 64
                project([Wr, Wk, Wv, Wwa, Wg], ti, off, n, zs)
                _r = os.environ.get("DBG_RW", "")
                yield
                if _r == "proj":
                    raise _Stop()
                if sample:
                    for i, zc in enumerate(zchunk):
                        k.dma(A["sh0"][:, i, :], I["st_sh"][l][:, zc * 128:(zc + 1) * 128].rearrange("b p -> p b"), slow=True)
                    for g in range(4):
                        for h2 in range(2):
                            hr = slice(h2 * 64, h2 * 64 + 64)
                            k.dma(A["sst"][hr, :, hr], I["st_rw"][l][g * 4:(g + 1) * 4, 2 * hp + h2, :, :].rearrange("b v k -> v b k"))
                        p_ = psum()
                        pv = p_.re("p (a b) -> p a b", a=4)
                        for q in range(4):
                            k.tr(pv[:, q, :], A["sst"][:, q, :], ident)
                        k.copy("dve", A["S32s"][:, g * 4:(g + 1) * 4, :], pv)
                        k.copy("act", A["Sbfs"][:, g * 4:(g + 1) * 4, :], A["S32s"][:, g * 4:(g + 1) * 4, :])
                for i in range(5):
                    z, s_, d = zs[i], ss[i], A["dd"]
                    mu = pc("rw_mu", l, zchunk[i])
                    if not sample:
                        k.tt("dve", d[:, 1:n], z[:, 0:n - 1], z[:, 1:n], ALU.subtract)
                        k.tt("pool", d[:, 0:1], A["zlast"][:, i:i + 1], z[:, 0:1], ALU.subtract)
                        k.copy("pool", A["zlast"][:, i:i + 1], z[:, n - 1:n])
                    else:
                        z3 = z[:, 0:64].re("p (b t) -> p b t", t=4)
                        d3 = d[:, 0:64].re("p (b t) -> p b t", t=4)
                        k.tt("dve", d3[:, :, 1:4], z3[:, :, 0:3], z3[:, :, 1:4], ALU.subtract)
                        k.tt("pool", d3[:, :, 0:1], A["sh0"][:, i, :].re("p (b o) -> p b o", o=1), z3[:, :, 0:1], ALU.subtract)
                    k.stt("dve", s_[:, 0:n], d[:, 0:n], mu, z[:, 0:n], ALU.mult, ALU.add)
                yield
                if _r == "shift":
                    raise _Stop()
                if lastp or sample:
                    for i, zc in enumerate(zchunk):
                        if i >= 3 and hp == 1:
                            continue
                        if sample:
                            k.dma(O["rw_sh_s"][l][:, zc * 128:(zc + 1) * 128].rearrange("b p -> p b"),
                                  zs[i][:, 0:64].re("p (b t) -> p b t", t=4)[:, :, 3], slow=True)
                        else:
                            k.dma(O["rw_sh_p"][l, zc * 128:(zc + 1) * 128].rearrange("(p o) -> p o", o=1),
                                  zs[i][:, n - 1:n], slow=True)
                sr, sk, sv, swa, sgg = (t_[:, 0:n] for t_ in ss)
                twx, sgb, sqb = A["twx"][:, 0:n], A["sgb"][:, 0:n], A["sqb"][:, 0:n]
                k.act(twx[0:64], swa[0:64], AF.Tanh)
                k.copy("pool", twx[64:128], swa[64:128])
                k.act(sgb, sgg, AF.Sigmoid)
                pW = psum()
                k.mm(pW[:, 0:n], [(wa2[0:64, l, hp * 128:(hp + 1) * 128], twx[0:64])])
                pA = psum()
                k.mm(pA[:, 0:n], [(wa2[64:128, l, hp * 128:(hp + 1) * 128], twx[64:128])])
                pGt = psum()
                k.mm(pGt[:, 0:n], [(gw2[:, l, hp * 128:(hp + 1) * 128], sgb)])
                logw, aa, gg = A["logw"][:, 0:n], A["aa"][:, 0:n], A["gg"][:, 0:n]
                k.act(logw, pW[:, 0:n], AF.Sigmoid, bias=pc("rw_w0", l, hp))
                k.act(logw, logw, AF.Identity, scale=WSCALE)
                k.act(aa, pA[:, 0:n], AF.Sigmoid, bias=pc("rw_a0", l, hp))
                k.copy("dve", gg, pGt[:, 0:n])
                yield
                if _r == "lora":
                    raise _Stop()
                kk, kp, bb, cum, Ep = (A[nm][:, 0:n] for nm in ["zr", "zk", "zv", "zwa", "zg"])
                A["Ep"] = A["zg"]
                t1, t2 = A["t1"][:, 0:n], A["t2"][:, 0:n]
                k.act(kk, sk, AF.Identity, scale=pc("rw_k_k", l, hp))
                k.act(sqb, kk, AF.Square)
                pN = psum()
                k.mm(pN[:, 0:n], [(blk1, sqb)])
                k.act(t1, pN[:, 0:n], AF.Ln, bias=EPS_TINY)
                k.act(t1, t1, AF.Exp, scale=-0.5)
                k.tt("dve", kk, kk, t1, ALU.mult)
                k.ts("pool", t2, aa, -1.0, pc("rw_k_a", l, hp), ALU.add, ALU.mult)
                k.stt("dve", kp, t2, 1.0, sk, ALU.add, ALU.mult)
                k.tt("dve", bb, kk, aa, ALU.mult)
                k.stt("dve", sqb, sr, pc("rw_r_k", l, hp), kp, ALU.mult, ALU.mult)
                pB = psum()
                k.mm(pB[:, 0:n], [(blk1, sqb)])
                k.copy("act", A["bsum"][:, 0:n], pB[:, 0:n])
                yield
                if _r == "kk":
                    raise _Stop()
                k.scan(cum, (rmask_s if sample else rmask)[:, 0:n], logw)
                En, Epm, Et = A["En"][:, 0:n], A["Epm"][:, 0:n], A["Et"][:, 0:n]
                k.act(Ep, cum, AF.Exp)
                k.act(En, cum, AF.Exp, scale=-1.0)
                k.tt("dve", Epm, cum, logw, ALU.subtract)
                k.act(Epm, Epm, AF.Exp)
                cl = 4 if sample else 64
                cum3 = cum.re("p (c t) -> p c t", t=cl)
                k.tt("pool", Et.re("p (c t) -> p c t", t=cl), cum3[:, :, cl - 1:cl].bc([128, n // cl, cl]), cum3, ALU.subtract)
                k.act(Et, Et, AF.Exp)
                AR, BK, BKd = A["AR"], A["BK"], A["BKd"]
                v3 = lambda t_: t_.re("p (c t) -> p c t", t=64)
                k.stt("dve", AR[:, 0:nch, 0:64], v3(kk), -1.0, v3(Epm), ALU.mult, ALU.mult)
                k.tt("pool", AR[:, 0:nch, 64:128], v3(sr), v3(Ep), ALU.mult)
                k.tt("dve", BK[:, 0:nch, 0, :], v3(bb), v3(En), ALU.mult)
                k.tt("pool", BK[:, 0:nch, 1, :], v3(kp), v3(En), ALU.mult)
                k.tt("dve", BKd[:, 0:nch, 0, :], v3(bb), v3(Et), ALU.mult)
                k.tt("pool", BKd[:, 0:nch, 1, :], v3(kp), v3(Et), ALU.mult)
                if sample:
                    k.tt("dve", A["ARm"][:, 0, :, :], AR[:, 0, 0:64].re("p (o t) -> p o t", o=1).bc([128, 16, 64]), colmask, ALU.mult)
                    k.tt("dve", A["ARm"][:, 1, :, :], AR[:, 0, 64:128].re("p (o t) -> p o t", o=1).bc([128, 16, 64]), colmask, ALU.mult)
                yield
                if _r == "prep":
                    raise _Stop()
                for c in range(nch):
                    p_ = psum()
                    k.tr(p_[0:64, 0:128], sv[:, c * 64:(c + 1) * 64], ident)
                    k.tr(p_[0:64, 128:256], BKd[:, c, 0, :], ident)
                    k.tr(p_[0:64, 256:384], BKd[:, c, 1, :], ident)
                    ev = "act" if c % 2 == 0 else "dve"
                    k.copy(ev, A["Vt"][c], p_[0:64, 0:128])
                    k.copy(ev, A["Bdt"][c], p_[0:64, 128:256])
                    k.copy(ev, A["Kdt"][c], p_[0:64, 256:384])
                yield
                if _r == "tr":
                    raise _Stop()
                yield from rwkv_stageA(A, list(range(nch)), sample)
                for c in range(nch):
                    yield from rwkv_stageB(A, c, sample)
                yield
                if _r == "units":
                    raise _Stop()
                oo = A["oo"][:, 0:n]
                mean, rstd = A["mean"][:, 0:n], A["rstd"][:, 0:n]
                head_stats(oo, n, EPS_GN, blk64, A["obb"], A["osq"], mean, rstd, t1)
                k.tt("pool", t2, oo, mean, ALU.subtract)
                k.tt("dve", t2, t2, rstd, ALU.mult)
                k.act(t2, t2, AF.Identity, bias=pc("rw_gn_b", l, hp), scale=pc("rw_gn_g", l, hp))
                k.tt("pool", t1, A["bsum"][:, 0:n], sv, ALU.mult)
                k.tt("dve", t2, t2, t1, ALU.add)
                outp = A["outp"][:, 0:n]
                k.tt("dve", outp, t2, gg, ALU.mult)
                if l == 0 and hp == 0 and ti == 0 and off == 0:
                    tap("rw_o", A["oo"])
                    if "rw_out" in TAPO:
                        k.tt("pool", t1, t2, gg, ALU.mult)
                        tap("rw_out", A["t1"])
                yield
                if _r == "gn":
                    raise _Stop()
                contrib(outp, Wo, ti, off, n)
                yield
                if _r == "tile1":
                    raise _Stop()
                if lastp:
                    p_ = psum()
                    k.tr(p_[:, 0:128], A["S32"], ident)
                    k.copy("act", A["sst"][:, 0, :], p_[:, 0:128])
                    for h2 in range(2):
                        hr = slice(h2 * 64, h2 * 64 + 64)
                        k.dma(O["rw_S_p"][l, 2 * hp + h2], A["sst"][hr, 0, hr])
                if sample:
                    for g in range(4):
                        p_ = psum()
                        pv = p_.re("p (a b) -> p a b", a=4)
                        for q in range(4):
                            k.tr(pv[:, q, :], A["S32s"][:, g * 4 + q, :], ident)
                        k.copy("dve", A["sst"], pv)
                        for h2 in range(2):
                            hr = slice(h2 * 64, h2 * 64 + 64)
                            k.dma(O["rw_S_s"][l][g * 4:(g + 1) * 4, 2 * hp + h2, :, :].rearrange("b v k -> v b k"), A["sst"][hr, :, hr])

        def hgrn_alloc(sample):
            A = {}
            nb_ = 64 if sample else 256
            ncq = 1 if sample else 4
            A["W"] = [ar([1024], BF16, name=f"hgW{i}") for i in range(5)]
            for nm in ["qs", "ff", "ii", "logf", "kf", "cum", "Ep", "En", "Et", "kd", "oo"]:
                A[nm] = ar([nb_], name=nm)
            A["t1"] = A["logf"]
            for nm in ["qt", "kt", "osq", "outp", "sog"]:
                A[nm] = ar([nb_], BF16, name=nm)
            A["It"] = [ar([128], BF16, parts=64, name=f"It{c}") for c in range(ncq)]
            A["Kdt"] = [ar([128], BF16, parts=64, name=f"Kdt{c}") for c in range(ncq)]
            A["Am"] = ar([64], BF16, parts=64, name="Am")
            A["S32"] = ar([128], name="S32")
            A["Sbf"] = ar([128], BF16, name="Sbf")
            if sample:
                A["S32s"] = ar([16, 128], name="S32s")
                A["Sbfs"] = ar([16, 128], BF16, name="Sbfs")
                A["qm"] = ar([16, 64], BF16, name="qm")
                A["Itm"] = ar([16, 128], BF16, parts=64, name="Itm")
            return A

        def hgrn_part(A, l, h, subtiles):
            base = 1024 + h * 128
            Wq = load_w_to(A["W"][0], win_cols(l, base), KC, 128)
            Wf = load_w_to(A["W"][1], win_cols(l, base + 512), KC, 128)
            Wi = load_w_to(A["W"][2], win_cols(l, base + 1024), KC, 128)
            Wog = load_w_to(A["W"][3], win_cols(l, base + 1536), KC, 128)
            Wo = load_w_to(A["W"][4], I["mix_w_out"][l][256 + h * 128:256 + (h + 1) * 128, :].rearrange("p (a b) -> p a b", a=KC), KC, 128)
            yield
            k.memset("pool", A["S32"], 0.0)
            k.memset("pool", A["Sbf"], 0.0)
            lb = lbc[:, l * 4 + h:l * 4 + h + 1]
            oml = omlc[:, l * 4 + h:l * 4 + h + 1]
            for (ti, off, n) in subtiles:
                sample = ti == 4
                lastp = (ti == 3 and off == 256)
                nch = n // 64
                project([Wq, Wf, Wi, Wog], ti, off, n, [A["qs"], A["ff"], A["ii"], A["sog"]],
                        [AF.Silu, AF.Sigmoid, None, AF.Silu])
                yield
                qs, ff, ii, sog = (A[nm][:, 0:n] for nm in ["qs", "ff", "ii", "sog"])
                logf, kf, cum, Ep, En, Et, kd, oo, t1 = (A[nm][:, 0:n] for nm in
                                                          ["logf", "kf", "cum", "Ep", "En", "Et", "kd", "oo", "t1"])
                qt, kt = A["qt"][:, 0:n], A["kt"][:, 0:n]
                if sample:
                    k.dma(A["S32s"], I["st_hg"][l][:, h, :, :].rearrange("b k v -> k b v"))
                    k.copy("act", A["Sbfs"], A["S32s"])
                k.ts("dve", ff, ff, oml, lb, ALU.mult, ALU.add)
                k.ts("pool", ff, ff, 1e-30, None, ALU.max)
                k.act(logf, ff, AF.Ln)
                k.act(kf, ff, AF.Identity, bias=epsc[:, 4:5], scale=-1.0)
                k.scan(cum, (rmask_s if sample else rmask)[:, 0:n], logf)
                k.act(Ep, cum, AF.Exp)
                k.act(En, cum, AF.Exp, scale=-1.0)
                cl = 4 if sample else 64
                cum3 = cum.re("p (c t) -> p c t", t=cl)
                k.tt("pool", Et.re("p (c t) -> p c t", t=cl), cum3[:, :, cl - 1:cl].bc([128, n // cl, cl]), cum3, ALU.subtract)
                k.act(Et, Et, AF.Exp)
                k.tt("dve", qt, qs, Ep, ALU.mult)
                k.tt("pool", kt, kf, En, ALU.mult)
                k.tt("dve", kd, kf, Et, ALU.mult)
                yield
                if sample:
                    k.tt("dve", A["qm"], qt.re("p (o t) -> p o t", o=1).bc([128, 16, 64]), colmask, ALU.mult)
                for c in range(nch):
                    cs = slice(c * 64, (c + 1) * 64)
                    p_ = psum()
                    k.tr(p_[0:64, 0:128], ii[:, cs], ident)
                    k.tr(p_[0:64, 128:256], kd[:, cs], ident)
                    ev = "act" if c % 2 == 0 else "dve"
                    k.copy(ev, A["It"][c], p_[0:64, 0:128])
                    k.copy(ev, A["Kdt"][c], p_[0:64, 128:256])
                mi = 1 if sample else 0
                yield
                for c in range(nch):
                    yield
                    cs = slice(c * 64, (c + 1) * 64)
                    pA = psum()
                    k.mm(pA[0:64, 0:64], [(kt[:, cs], qt[:, cs])])
                    k.tt("dve", A["Am"], pA[0:64, 0:64], gmask[0:64, mi, 64:128], ALU.mult)
                    pO = psum()
                    if sample:
                        pairs = [(A["Sbfs"][:, b, :], A["qm"][:, b, :]) for b in range(16)]
                    else:
                        pairs = [(A["Sbf"], qt[:, cs])]
                    pairs.append((A["It"][c], A["Am"]))
                    k.mm(pO[:, 0:64], pairs)
                    k.copy("act", oo[:, cs], pO[:, 0:64])
                    if not sample:
                        pS = psum()
                        k.mm(pS[:, 0:128], [(A["Kdt"][c], A["It"][c])])
                        k.stt("dve", A["S32"], A["S32"], Ep[:, c * 64 + 63:c * 64 + 64], pS[:, 0:128], ALU.mult, ALU.add)
                        k.copy("act", A["Sbf"], A["S32"])
                    else:
                        k.tt("dve", A["Itm"], A["It"][0].re("p (o v) -> p o v", o=1).bc([64, 16, 128]),
                             rowmask.re("p (b o) -> p b o", o=1).bc([64, 16, 128]), ALU.mult)
                        Eps = Ep.re("p (b t) -> p b t", t=4)
                        for g in range(4):
                            pS = psum()
                            for b4 in range(4):
                                b = g * 4 + b4
                                k.mm(pS[:, b4 * 128:(b4 + 1) * 128], [(A["Kdt"][0], A["Itm"][:, b, :])])
                            bs_ = slice(g * 4, g * 4 + 4)
                            k.tt("dve", A["S32s"][:, bs_, :], A["S32s"][:, bs_, :], Eps[:, bs_, 3:4].bc([128, 4, 128]), ALU.mult)
                            k.tt("dve", A["S32s"][:, bs_, :], A["S32s"][:, bs_, :], pS.re("p (b v) -> p b v", b=4), ALU.add)
                yield
                osq = A["osq"][:, 0:n]
                k.act(osq, oo, AF.Square)
                pq = psum()
                k.mm(pq[:, 0:n], [(onesD, osq)])
                k.act(t1, pq[:, 0:n], AF.Ln, bias=EPS_RMS)
                k.act(t1, t1, AF.Exp, scale=-0.5)
                k.tt("dve", t1, t1, oo, ALU.mult)
                outp = A["outp"][:, 0:n]
                k.stt("dve", outp, t1, pc("hg_norm_g", l, h), sog, ALU.mult, ALU.mult)
                yield
                contrib(outp, Wo, ti, off, n)
                yield
                if lastp:
                    k.dma(O["hg_S_p"][l, h], A["S32"])
                if sample:
                    k.dma(O["hg_S_s"][l][:, h, :, :].rearrange("b k v -> k b v"), A["S32s"])

        def cm_alloc(sample):
            A = {}
            nb_ = 64 if sample else 256
            A["W"] = [ar([1024], BF16, name=f"cmW{i}") for i in range(3)]
            if sample:
                A["stg"] = stg[1]
                for nm in ["uu", "vv", "mean", "rstd"]:
                    A[nm] = ar([nb_], name=nm)
            else:
                A["stg"] = ar([1024], name="cmstg")
                for i_, nm in enumerate(["uu", "vv", "mean", "rstd"]):
                    A[nm] = T(A["stg"].ap[:, i_ * 256:(i_ + 1) * 256], Buf(nm))
            for nm in ["vb", "vsq", "outp"]:
                A[nm] = ar([nb_], BF16, name=nm)
            A["Vt"] = ar([128], BF16, name="Vt")
            A["Vt32"] = ar([128], name="Vt32", parts=64)
            return A

        def cm_part(A, l, hp, subtiles):
            Wu = load_w_to(A["W"][0], win_cols(l, 3072 + hp * 128), KC, 128, A["stg"])
            Wv = load_w_to(A["W"][1], win_cols(l, 3328 + hp * 128), KC, 128, A["stg"])
            Wo = load_w_to(A["W"][2], I["mix_w_out"][l][768 + hp * 128:768 + (hp + 1) * 128, :].rearrange("p (a b) -> p a b", a=KC), KC, 128, A["stg"])
            yield
            for (ti, off, n) in subtiles:
                sample = ti == 4
                project([Wu, Wv], ti, off, n, [A["uu"], A["vv"]], [AF.Gelu, AF.Gelu])
                yield
                uu, vv, mean, rstd = (A[nm][:, 0:n] for nm in ["uu", "vv", "mean", "rstd"])
                head_stats(vv, n, EPS_LN, blk64, A["vb"], A["vsq"], mean, rstd, rstd, one_bank=sample)
                k.tt("pool", vv, vv, mean, ALU.subtract)
                k.tt("dve", vv, vv, rstd, ALU.mult)
                k.act(vv, vv, AF.Identity, bias=pc("cm_ln_b", l, hp), scale=pc("cm_ln_g", l, hp))
                outp = A["outp"][:, 0:n]
                yield
                nb = 1 if sample else 2
                bl = 64 if sample else 128
                for cb in range(nb):
                    cs = slice(cb * bl, (cb + 1) * bl)
                    p_ = psum()
                    k.tr(p_[0:bl, 0:128], vv[:, cs], ident)
                    k.copy("act", A["Vt"][0:bl, :], p_[0:bl, 0:128])
                    if sample:
                        k.copy("act", A["Vt32"], p_[0:64, 0:128])
                        k.dma(O["cm_v_s"][l][:, :, hp * 128:(hp + 1) * 128].rearrange("b t d -> (b t) d"), A["Vt32"])
                    for h2 in range(2):
                        hr = slice(h2 * 64, h2 * 64 + 64)
                        li = l * 4 + 2 * hp + h2
                        pM = psum()
                        if sample:
                            k.mm(pM[:, 0:64], [(A["Vt"][0:64, :], wcs[:, li, :]), (ones_row[0:1, :], bss[0:1, li, :])])
                        else:
                            k.mm(pM[:, 0:128], [(A["Vt"], wct[:, li, :]),
                                                (ones_row[0:1, :], bsrow[0:1, li * 128:(li + 1) * 128])])
                        k.tt("dve", outp[hr, cs], uu[hr, cs], pM[hr, 0:bl], ALU.mult)
                yield
                contrib(outp, Wo, ti, off, n)
                yield

        def chk(name):
            if stop_after == name:
                raise _Stop()

        def run_lanes(gens, banks):
            state["banks"] = banks
            lists = []
            try:
                for i, g in enumerate(gens):
                    state["lane"] = i
                    S.defer = []
                    for _ in g:
                        pass
                    lists.append(S.defer)
            finally:
                S.defer = None
                state["banks"] = None
                state["lane"] = 0
            tot = [len(x) for x in lists]
            pos = [0] * len(lists)
            eng_free = {e: 0.0 for e in ALLENG}
            bw, br = {}, {}
            HOP = 0.15
            while True:
                best, bi, bstart = None, -1, 0.0
                for i in range(len(lists)):
                    if pos[i] >= tot[i]:
                        continue
                    eng, fn, rd, wr, dma, cost, single = lists[i][pos[i]]
                    st_t = eng_free[eng]
                    for b_ in rd:
                        st_t = max(st_t, bw.get(id(b_), 0.0) + HOP)
                    for b_ in wr:
                        st_t = max(st_t, bw.get(id(b_), 0.0) + HOP, br.get(id(b_), 0.0) + HOP)
                    key = (round(st_t, 1), pos[i] / tot[i])
                    if best is None or key < best:
                        best, bi, bstart = key, i, st_t
                if bi < 0:
                    break
                eng, fn, rd, wr, dma, cost, single = lists[bi][pos[bi]]
                fin = bstart + cost
                eng_free[eng] = bstart + (0.1 if dma else cost)
                for b_ in rd:
                    br[id(b_)] = max(br.get(id(b_), 0.0), fin)
                for b_ in wr:
                    bw[id(b_)] = fin
                    br[id(b_)] = 0.0
                S.op(eng, fn, rd, wr, dma, cost, single)
                pos[bi] += 1

        def main_flow():
          chk("input")
          chk("input_only")
          chk("consts")
          chk("cm1")
          chk("cmall")
          alloc_ffn_ln(reset=False)
          for l in range(L):
            ffn(l, "ffn1_w_in", "ffn1_w_out", "ln1_g", "ln1_b")
            if l == 0:
                tap("x_ln1", x[0])
            chk("ffn1")
            MTP = [m_ for m_ in MT if m_[0] != 4]
            MTS = [m_ for m_ in MT if m_[0] == 4]
            def lane_rw(A_, subt, hps=(0, 1)):
                for hp in hps:
                    yield from rwkv_part(A_, l, hp, subt)

            def lane_hg(B_, hs, subt):
                for h in hs:
                    yield from hgrn_part(B_, l, h, subt)

            def lane_cm(C_, hp, subt):
                yield from cm_part(C_, l, hp, subt)

            arena_reset()
            A1 = rwkv_alloc(False)
            B1 = hgrn_alloc(False)
            print("[arena cols A1]", state["ar"], "of", RCOLS, flush=True)
            run_lanes([lane_rw(A1, MTP), lane_hg(B1, (0, 1, 2, 3), MTP)], [[0, 1, 2, 3, 4], [5, 6, 7]])
            arena_reset()
            A2 = rwkv_alloc(True)
            C0, C1 = cm_alloc(False), cm_alloc(False)
            CS = cm_alloc(True)
            print("[arena cols A2]", state["ar"], "of", RCOLS, flush=True)

            def lane_cm_s(C_):
                for hp_ in range(2):
                    yield from cm_part(C_, l, hp_, MTS)

            run_lanes([lane_rw(A2, MTS, (0,)), lane_cm(C0, 0, MTP), lane_cm(C1, 1, MTP), lane_cm_s(CS)],
                      [[0, 1, 2], [3, 4], [5, 6], [7]])
            arena_reset()
            A3 = rwkv_alloc(True)
            B3 = hgrn_alloc(True)
            print("[arena cols A3]", state["ar"], "of", RCOLS, flush=True)
            run_lanes([lane_rw(A3, MTS, (1,)), lane_hg(B3, (0, 1, 2, 3), MTS)], [[0, 1, 2], [3, 4, 5, 6, 7]])
            chk("cm")
            alloc_ffn_ln()
            layer_norm(l, "ln2_g", "ln2_b")
            if l == 0:
                tap("x_ln2", x[0])
            chk("ln2")
            ffn(l, "ffn2_w_in", "ffn2_w_out", "ln3_g", "ln3_b")
            chk("layer0")

        try:
            main_flow()
        except _Stop:
            pass

        instage = stg
        for bi in range(17 if stop_after not in ("consts", "cm1", "cmall", "input_only") else 0):
            sgi = instage[bi % 2]
            if bi < 16:
                nrow, ti, c0 = 128, bi // 4, (bi % 4) * 128
            else:
                nrow, ti, c0 = 64, 4, 0
            for g in range(2):
                p_ = psum()
                for q in range(4):
                    kc = g * 4 + q
                    k.tr(p_[0:nrow, q * 128:(q + 1) * 128], x[ti][:, kc, c0:c0 + nrow], ident)
                k.copy("act" if g == 0 else "dve", sgi[0:nrow, g * 512:(g + 1) * 512], p_[0:nrow, :])
            if bi < 16:
                k.dma(O["y_p"][bi * 128:(bi + 1) * 128, :], sgi)
            else:
                k.dma(O["y_s"], sgi[0:64, :])
        S.barrier()
        S.replay(block)
        print(f"[build] ops={S.n_ops} waits={S.n_waits}", flush=True)
    return nc


_NC_CACHE = {}


def make_in_maps(inputs):
    f32 = lambda a: np.ascontiguousarray(np.asarray(a, dtype=np.float32))
    consts = host_consts()
    cols = []
    for name, nchunk in PCOL_SPEC:
        a = f32(inputs[name]).reshape(L, nchunk, 128)
        cols.append(a.reshape(L * nchunk, 128))
    pcols = np.ascontiguousarray(np.concatenate(cols, 0).T)
    shared = {n: f32(inputs[n]) for n in ["ffn1_w_in", "ffn1_w_out", "mix_w_in", "mix_w_out", "ffn2_w_in",
                                           "ffn2_w_out", "rw_w_w2", "rw_a_w2", "rw_g_w2", "cm_ws", "cm_bs"]}
    shared["pcols"] = pcols
    for n, v in consts.items():
        shared["c_" + n] = f32(v)
    xp = f32(inputs["x_prompt"])
    xs = f32(inputs["x_sample"])
    srw = f32(inputs["state_rwkv"])
    ssh = f32(inputs["state_rwkv_shift"])
    shg = f32(inputs["state_hgrn"])
    maps = []
    for c in range(NCORES):
        m = dict(shared)
        b0 = c * NSB
        m["xp"] = xp[c]
        m["xs"] = np.ascontiguousarray(xs[b0:b0 + NSB].reshape(NSB * NST, D))
        m["st_rw"] = np.ascontiguousarray(srw[:, b0:b0 + NSB])
        m["st_sh"] = np.ascontiguousarray(ssh[:, b0:b0 + NSB])
        m["st_hg"] = np.ascontiguousarray(shg[:, b0:b0 + NSB])
        maps.append(m)
    return maps


def gather(results):
    cat = lambda name, axis: np.concatenate([np.asarray(r[name], dtype=np.float32) for r in results], axis=axis)
    y_p = np.stack([np.asarray(r["y_p"], np.float32) for r in results], 0)
    y_s = cat("y_s", 0).reshape(NCORES * NSB, NST, D)
    rw_S_p = np.stack([np.asarray(r["rw_S_p"], np.float32) for r in results], 1)
    rw_sh_p = np.stack([np.asarray(r["rw_sh_p"], np.float32) for r in results], 1)
    hg_S_p = np.stack([np.asarray(r["hg_S_p"], np.float32) for r in results], 1)
    rw_S_s = cat("rw_S_s", 1)
    rw_sh_s = cat("rw_sh_s", 1)
    hg_S_s = cat("hg_S_s", 1)
    cm_v_s = cat("cm_v_s", 1)
    return (y_p, y_s, rw_S_p, rw_sh_p, hg_S_p, rw_S_s, rw_sh_s, hg_S_s, cm_v_s)


def kernel(**inputs):
    if "nc" not in _NC_CACHE:
        _NC_CACHE["nc"] = build()
    nc = _NC_CACHE["nc"]
    maps = make_in_maps(inputs)
    res = run_bass_kernel_spmd(nc, maps, core_ids=list(range(NCORES)))
    return gather(res.results)
```
